# Optimizing a Trainium2 kernel written in Bass

```python
import math
import jax, jax.numpy as jnp
from jax import lax
import numpy as np

D_MODEL = 1024
BATCH = 2
SEQ = 16384
DEPTH = 1

MIX_WIDTH = D_MODEL
CONV_WIDTH = MIX_WIDTH // 2
ATTN_HEADS = 8
HEAD_DIM = (MIX_WIDTH - CONV_WIDTH) // ATTN_HEADS
ATTN_WIDTH = ATTN_HEADS * HEAD_DIM
CONV_K = 3
DILATED_CONFIGS = ((128, 1), (512, 4), (2048, 16))
Q_BLOCK = 128
NUM_BUCKETS = 32
MAX_DISTANCE = 1024
EPS = 1e-6
NEG = -1e30

kernel_name = "hybrid_shortconv_dilated_attn_block"


def rmsnorm(x, w):
    xf = x.astype(jnp.float32)
    y = xf * lax.rsqrt(jnp.mean(xf * xf, axis=-1, keepdims=True) + EPS)
    return (y * w.astype(jnp.float32)).astype(x.dtype)


def t5_bucket(rel):
    half_b = NUM_BUCKETS // 2
    max_exact = half_b // 2
    ret = jnp.where(rel > 0, half_b, 0)
    n = jnp.abs(rel)
    nf = jnp.maximum(n, 1).astype(jnp.float32)
    large = max_exact + (jnp.log(nf / max_exact) / math.log(MAX_DISTANCE / max_exact)
                         * (half_b - max_exact)).astype(jnp.int32)
    large = jnp.minimum(large, half_b - 1)
    return ret + jnp.where(n < max_exact, n, large)


def dilated_window_attention(q, k, v, rel_bias, window, dilation):
    b, s, nh, dh = q.shape
    half = window // (2 * dilation)
    length = s // dilation
    n_blk = -(-length // Q_BLOCK)
    padded = n_blk * Q_BLOCK
    kb_len = Q_BLOCK + 2 * half
    bd = b * dilation

    def to_residue(t):
        t = t.reshape(b, length, dilation, nh, dh)
        return t.transpose(0, 2, 3, 1, 4).reshape(bd, nh, length, dh)

    qr = jnp.pad(to_residue(q), ((0, 0), (0, 0), (0, padded - length), (0, 0)))
    qr = qr.reshape(bd, nh, n_blk, Q_BLOCK, dh)
    pad_kv = ((0, 0), (0, 0), (half, padded - length + half), (0, 0))
    kr = jnp.pad(to_residue(k), pad_kv)
    vr = jnp.pad(to_residue(v), pad_kv)

    key_idx = jnp.arange(n_blk)[:, None] * Q_BLOCK + jnp.arange(kb_len)[None, :]
    kblk = kr[:, :, key_idx]
    vblk = vr[:, :, key_idx]

    logits = jnp.einsum('bhnqd,bhnkd->bhnqk', qr, kblk,
                        preferred_element_type=jnp.float32) * (dh ** -0.5)

    rel = jnp.arange(kb_len)[None, :] - half - jnp.arange(Q_BLOCK)[:, None]
    band = jnp.abs(rel) <= half
    buckets = t5_bucket(jnp.clip(rel, -half, half) * dilation)
    bias = rel_bias[buckets].astype(jnp.float32).transpose(2, 0, 1)
    key_pos = key_idx - half
    key_ok = (key_pos >= 0) & (key_pos < length)
    mask = band[None, :, :] & key_ok[:, None, :]
    logits = jnp.where(mask[None, None], logits + bias[None, :, None], NEG)

    m = jnp.max(logits, axis=-1, keepdims=True)
    p = jnp.exp(logits - m)
    denom = jnp.sum(p, axis=-1, keepdims=True)
    o = jnp.einsum('bhnqk,bhnkd->bhnqd', p, vblk.astype(jnp.float32)) / denom
    lse = (m + jnp.log(denom))[..., 0]

    o = o.reshape(bd, nh, padded, dh)[:, :, :length]
    o = o.reshape(b, dilation, nh, length, dh).transpose(0, 3, 1, 2, 4).reshape(b, s, nh, dh)
    lse = lse.reshape(bd, nh, padded)[:, :, :length]
    lse = lse.reshape(b, dilation, nh, length).transpose(0, 3, 1, 2).reshape(b, s, nh)
    return o, lse


def short_gated_conv(u, gate_b, gate_c, conv_w, conv_b):
    pad = (CONV_K - 1) // 2
    z = lax.conv_general_dilated(gate_c * u, conv_w[:, None, :].astype(u.dtype),
                                 window_strides=(1,), padding=((pad, pad),),
                                 dimension_numbers=('NWC', 'WIO', 'NWC'),
                                 feature_group_count=CONV_WIDTH)
    return gate_b * (z + conv_b.astype(u.dtype))


def setup_inputs(seed: int = 0) -> dict:
    key = jax.random.key(seed)
    ks = jax.random.split(key, 9)
    proj_cols = 4 * CONV_WIDTH + 4 * ATTN_WIDTH
    x = jax.random.normal(ks[0], (BATCH, SEQ, D_MODEL), jnp.float32)
    norm_w = 1.0 + 0.05 * jax.random.normal(ks[1], (D_MODEL,), jnp.float32)
    w_in = jax.random.normal(ks[2], (D_MODEL, proj_cols), jnp.float32) * D_MODEL ** -0.5
    conv_w = jax.random.normal(ks[3], (CONV_K, CONV_WIDTH), jnp.float32) * CONV_K ** -0.5
    conv_b = 0.01 * jax.random.normal(ks[4], (CONV_WIDTH,), jnp.float32)
    q_norm_w = 1.0 + 0.05 * jax.random.normal(ks[5], (HEAD_DIM,), jnp.float32)
    k_norm_w = 1.0 + 0.05 * jax.random.normal(ks[6], (HEAD_DIM,), jnp.float32)
    rel_bias = 0.5 * jax.random.normal(ks[7], (NUM_BUCKETS, ATTN_HEADS), jnp.float32)
    w_out = jax.random.normal(ks[8], (MIX_WIDTH, D_MODEL), jnp.float32) * MIX_WIDTH ** -0.5
    return {"x": x, "norm_w": norm_w, "w_in": w_in, "conv_w": conv_w, "conv_b": conv_b,
            "q_norm_w": q_norm_w, "k_norm_w": k_norm_w, "rel_bias": rel_bias, "w_out": w_out}


def reference(x, norm_w, w_in, conv_w, conv_b, q_norm_w, k_norm_w, rel_bias, w_out):
    b, s, _ = x.shape
    for _layer in range(DEPTH):
        h = rmsnorm(x, norm_w)
        proj = jnp.einsum('bsd,de->bse', h, w_in)
        c, a = CONV_WIDTH, ATTN_WIDTH
        splits = [c, 2 * c, 3 * c, 4 * c, 4 * c + a, 4 * c + 2 * a, 4 * c + 3 * a]
        u, g_b, g_c, z_conv, q, k, v, z_attn = jnp.split(proj, splits, axis=-1)

        y_conv = short_gated_conv(u, g_b, g_c, conv_w, conv_b) * jax.nn.silu(z_conv)

        q = rmsnorm(q.reshape(b, s, ATTN_HEADS, HEAD_DIM), q_norm_w)
        k = rmsnorm(k.reshape(b, s, ATTN_HEADS, HEAD_DIM), k_norm_w)
        v = v.reshape(b, s, ATTN_HEADS, HEAD_DIM)
        outs, lses = [], []
        for window, dilation in DILATED_CONFIGS:
            o_i, lse_i = dilated_window_attention(q, k, v, rel_bias, window, dilation)
            outs.append(o_i)
            lses.append(lse_i)
        mix = jax.nn.softmax(jnp.stack(lses, axis=0), axis=0)
        o = jnp.einsum('gbsh,gbshd->bshd', mix, jnp.stack(outs, axis=0))
        y_attn = o.reshape(b, s, ATTN_WIDTH).astype(x.dtype) * jax.nn.silu(z_attn)

        y = jnp.concatenate([y_conv, y_attn], axis=-1)
        x = x + jnp.einsum('bse,ed->bsd', y, w_out)
    return x
```

```python
import contextlib
import math
import numpy as np
import concourse.bass as bass
import concourse.mybir as mybir
from concourse.bass_utils import run_bass_kernel_spmd

F32 = mybir.dt.float32
BF16 = mybir.dt.bfloat16
AF = mybir.ActivationFunctionType
ALU = mybir.AluOpType

NCORES = 8
SEQ = 16384
DM = 1024
OWN = 4096
HALO = 1024
LOC = OWN + 2 * HALO
NG = LOC // 512
NEG = -30000.0
EPS = 1e-6
DILS = (1, 4, 16)


class Sched:
    EPOCH = 12000

    def __init__(self):
        self.ops = []
        self.last_writer = {}
        self.readers = {}
        self.last_on_eng = {}
        self.last_in_group = {}

    def op(self, eng, fn, reads=(), writes=(), dma_group=None, extra_deps=()):
        idx = len(self.ops)
        deps = set(extra_deps)
        for k in reads:
            w = self.last_writer.get(k)
            if w is not None:
                deps.add(w)
        for k in writes:
            w = self.last_writer.get(k)
            if w is not None:
                deps.add(w)
            deps.update(self.readers.get(k, ()))
        deps.discard(idx)
        for k in reads:
            self.readers.setdefault(k, []).append(idx)
        for k in writes:
            self.last_writer[k] = idx
            self.readers[k] = []
        self.ops.append(dict(eng=eng, fn=fn, deps=deps, dma_group=dma_group, signal=None))
        if dma_group is None:
            self.last_on_eng[eng] = idx
        else:
            self.last_in_group[dma_group] = idx
        return idx

    def barrier(self, fn, key):
        deps = set(self.last_on_eng.values()) | set(self.last_in_group.values())
        return self.op("pool", fn, writes=[key], extra_deps=deps)

    @staticmethod
    def _skip(od, o):
        return (od["eng"] == "pe" and o["eng"] == "pe" and od["dma_group"] is None
                and o["dma_group"] is None)

    def emit(self, nc, final_groups=()):
        ops = self.ops
        needed = set()
        for o in ops:
            for d in o["deps"]:
                if not self._skip(ops[d], o):
                    needed.add(d)
        counters = {}
        ecount = {}
        group_ops = {}
        for i, o in enumerate(ops):
            if o["dma_group"] is not None:
                key = "d_" + o["dma_group"]
                counters[key] = counters.get(key, 0) + 16
                o["signal"] = (key, counters[key], 16)
                group_ops.setdefault(o["dma_group"], []).append(i)
            elif i in needed:
                n = ecount.get(o["eng"], 0)
                ecount[o["eng"]] = n + 1
                key = "e_%s_%d" % (o["eng"], n // self.EPOCH)
                counters[key] = counters.get(key, 0) + 1
                o["signal"] = (key, counters[key], 1)
        keys = sorted(counters.keys())
        with contextlib.ExitStack() as es:
            sems = {k: es.enter_context(nc.semaphore(k)) for k in keys}
            streams = {}
            for i, o in enumerate(ops):
                streams.setdefault(o["eng"], []).append(i)
            plans = {}
            import bisect
            for e, lst in streams.items():
                waited = {}
                plan = []
                for i in lst:
                    o = ops[i]
                    w = {}
                    for d in o["deps"]:
                        od = ops[d]
                        if od["signal"] is None or self._skip(od, o):
                            continue
                        if od["dma_group"] is not None:
                            gl = group_ops[od["dma_group"]]
                            j = bisect.bisect_left(gl, i) - 1
                            od = ops[gl[j]]
                        k, c, _ = od["signal"]
                        if waited.get(k, 0) >= c:
                            continue
                        w[k] = max(w.get(k, 0), c)
                    for k, c in w.items():
                        waited[k] = c
                    plan.append((i, sorted(w.items())))
                plans[e] = plan
            final = [("d_" + g, counters["d_" + g]) for g in final_groups]
            engmap = {"pe": "tensor", "act": "scalar", "dve": "vector", "pool": "gpsimd",
                      "sp": "sync"}
            with nc.Block() as block:
                for e in ["sp", "pe", "act", "dve", "pool"]:
                    plan = plans.get(e, [])

                    def body(engine, plan=plan, e=e):
                        for i, ws in plan:
                            for k, c in ws:
                                engine.wait_ge(sems[k], c)
                            o = ops[i]
                            ins = o["fn"](engine)
                            if o["signal"] is not None:
                                k, c, inc = o["signal"]
                                ins.then_inc(sems[k], inc)
                        if e == "sp":
                            for k, c in final:
                                engine.wait_ge(sems[k], c)
                    getattr(block, engmap[e])(body)
        return counters


def attn_sequences(di):
    d = DILS[di]
    seqs = []
    if d == 16:
        t0s = [0, 128, 256]
        own_lo, own_hi = 64, 320
    else:
        own_lo, own_hi = HALO // d, (HALO + OWN) // d
        t0s = [128 * j + 64 for j in range(own_lo // 128 - 1, own_hi // 128)]
    for res in range(d):
        tiles = []
        for T0 in t0s:
            useA = own_lo <= T0 - 64 and T0 + 64 <= own_hi
            useB = own_lo <= T0 + 64 and T0 + 192 <= own_hi
            lo_tok = res + d * T0
            hi_tok = res + d * (T0 + 127)
            masked = lo_tok < HALO or hi_tok >= HALO + OWN
            tiles.append((T0, useA, useB, masked))
        seqs.append((res, tiles))
    return seqs


def masked_tile_index():
    idx = {}
    n = 0
    for di in range(3):
        for res, tiles in attn_sequences(di):
            for (T0, useA, useB, masked) in tiles:
                if masked:
                    idx[(di, res, T0)] = n
                    n += 1
    return idx, n


def t5_bucket_np(rel):
    nb = 32
    half_b = nb // 2
    max_exact = half_b // 2
    ret = np.where(rel > 0, half_b, 0)
    n = np.abs(rel)
    nf = np.maximum(n, 1).astype(np.float32)
    large = max_exact + (np.log(nf / np.float32(max_exact)) / np.float32(math.log(1024 / max_exact))
                         * (half_b - max_exact)).astype(np.int32)
    large = np.minimum(large, half_b - 1)
    return ret + np.where(n < max_exact, n, large)


def build_program():
    nc = bass.Bass("TRN2", target_bir_lowering=False)
    MIDX, NM = masked_tile_index()

    xh_d = nc.dram_tensor("xh", [LOC, DM], F32, kind="ExternalInput").ap()
    win_d = nc.dram_tensor("w_in", [DM, 4096], F32, kind="ExternalInput").ap()
    wout_d = nc.dram_tensor("w_out", [DM, DM], F32, kind="ExternalInput").ap()
    cst_d = nc.dram_tensor("cst", [128, 32], F32, kind="ExternalInput").ap()
    nwb_d = nc.dram_tensor("nwb", [128, DM], F32, kind="ExternalInput").ap()
    vb_d = nc.dram_tensor("vb", [128, NM], F32, kind="ExternalInput").ap()
    bias_d = nc.dram_tensor("biasT", [3, 128, 8, 256], F32, kind="ExternalInput").ap()
    out_d = nc.dram_tensor("out", [OWN, DM], F32, kind="ExternalOutput").ap()
    Qd = nc.dram_tensor("Qd", [128, 4, OWN], BF16).ap()
    Kd = nc.dram_tensor("Kd", [128, 4, LOC], BF16).ap()
    Zd = nc.dram_tensor("Zd", [128, 4, OWN], BF16).ap()
    Yd = nc.dram_tensor("Yd", [128, 4, OWN], BF16).ap()
    Vd_t = nc.dram_tensor("Vd", [LOC, 512], BF16)
    Vd = Vd_t.ap()

    S = Sched()
    base = (nc.sbuf_base + 31) // 32 * 32
    top = nc.sbuf_top
    cur = [base]

    def alloc(name, shape, dt, at=None):
        nbytes = int(np.prod(shape[1:])) * (4 if dt == F32 else 2)
        nbytes = (nbytes + 31) // 32 * 32
        if at is None:
            off = cur[0]
            cur[0] += nbytes
        else:
            off = at
        assert off + nbytes <= top, (name, off, nbytes, top)
        return nc.alloc_sbuf_tensor_at(name, list(shape), dt, offset=off), off + nbytes

    ident, _ = alloc("ident", [128, 128], BF16)
    ones, _ = alloc("ones", [128, 128], BF16)
    blk, _ = alloc("blk", [128, 128], BF16)
    cst, _ = alloc("cst_sb", [128, 32], F32)
    vb, _ = alloc("vb_sb", [128, NM], F32)
    stat, _ = alloc("stat", [128, 64], F32)
    gq, _ = alloc("gq", [128, 2], F32)
    dummy, _ = alloc("dummy", [128, 8], F32)
    nwb, _ = alloc("nwb_sb", [128, DM], F32)
    P0 = cur[0]

    psall = nc.alloc_psum_tensor("psall", [128, 4096], F32)
    psum = [psall[:, 512 * i:512 * i + 512] for i in range(8)]

    S.op("sp", lambda e: e.dma_start(out=cst[:], in_=cst_d[:, :]), writes=["cst"], dma_group="cst")
    S.op("sp", lambda e: e.dma_start(out=vb[:], in_=vb_d[:, :]), writes=["vb"], dma_group="vbl")
    S.op("sp", lambda e: e.dma_start(out=nwb[:], in_=nwb_d[:, :]), writes=["nwb"], dma_group="nwbl")
    S.op("pool", lambda e: e.memset(ident[:], 1.0), writes=["ident"])
    S.op("pool", lambda e: e.affine_select(out=ident[:], in_=ident[:], pattern=[[-1, 128]],
                                           compare_op=ALU.is_equal, fill=0.0, base=0,
                                           channel_multiplier=1),
         reads=["ident"], writes=["ident"])
    S.op("pool", lambda e: e.memset(ones[:], 1.0), writes=["ones"])
    S.op("pool", lambda e: e.memset(blk[:], 0.0), writes=["blk"])
    S.op("pool", lambda e: e.memset(blk[0:64, 0:64], 1.0), reads=["blk"], writes=["blk"])
    S.op("pool", lambda e: e.memset(blk[64:128, 64:128], 1.0), reads=["blk"], writes=["blk"])
    S.op("dve", lambda e: e.tensor_scalar(out=gq[:, 0:1], in0=cst[:, 24:25], scalar1=0.125,
                                          scalar2=None, op0=ALU.mult),
         reads=["cst"], writes=["gq"])

    cur[0] = P0
    Wbf, _ = alloc("Wbf", [128, 8, 4096], BF16)
    xs = [alloc("xs%d" % i, [128, 1024], F32)[0] for i in range(4)]
    xn = [alloc("xn%d" % i, [128, 1024], BF16)[0] for i in range(4)]
    hT = [alloc("hT%d" % i, [128, 8, 512], BF16)[0] for i in range(2)]
    cu_ext, _ = alloc("cu_ext", [128, 4, 514], F32)
    gz_ext, _ = alloc("gz_ext", [128, 4, 513], F32)
    u_sb = [alloc("u_sb%d" % i, [128, 512], F32)[0] for i in range(2)]
    sz_sb = [alloc("sz_sb%d" % i, [128, 512], F32)[0] for i in range(2)]
    tcv = [alloc("tcv%d" % i, [128, 512], F32)[0] for i in range(2)]
    qsq = [alloc("qsq%d" % i, [128, 512], BF16)[0] for i in range(2)]
    rr = [alloc("rr%d" % i, [128, 512], F32)[0] for i in range(2)]
    stQ = [alloc("stQ%d" % i, [128, 4, 512], BF16)[0] for i in range(2)]
    stK = [alloc("stK%d" % i, [128, 4, 512], BF16)[0] for i in range(2)]
    stZ = [alloc("stZ%d" % i, [128, 4, 512], BF16)[0] for i in range(1)]
    stY = [alloc("stY%d" % i, [128, 4, 512], BF16)[0] for i in range(2)]
    stV = [alloc("stV%d" % i, [128, 4, 512], BF16)[0] for i in range(1)]
    hT_tail, _ = alloc("hT_tail", [128, 8, 16], BF16)
    PA_END = cur[0]
    PF0 = (top - (LOC + OWN + 48 * 128) * 2) // 32 * 32
    assert PA_END <= PF0, (PA_END, PF0)
    KT_A, pf1 = alloc("KT_A", [128, LOC], BF16, at=PF0)
    QT_A, pf2 = alloc("QT_A", [128, OWN], BF16, at=pf1)
    Vt0, pf3 = alloc("Vt0", [128, 48, 128], BF16, at=pf2)
    KD_KEYS = ["Kd_g%d" % g for g in range(NG)]
    QD_KEYS = ["Qd_g%d" % g for g in range(OWN // 512)]
    VD_KEYS = ["Vd_g%d" % g for g in range(NG)]

    win_v = win_d.rearrange("(kc p) n -> p kc n", p=128)
    wb_order = [6, 5, 4, 0, 2, 1, 3, 7]

    def emit_weights(lo, hi):
        for bi in range(lo, hi):
            b = wb_order[bi]
            S.op("pool", lambda e, b=b: e.dma_start(out=Wbf[:, :, 512 * b:512 * b + 512],
                                                    in_=win_v[:, :, 512 * b:512 * b + 512]),
                 writes=["Wbf_c%d" % b], dma_group="wld%d" % b)

    def wkeys(c0, c1):
        return ["Wbf_c%d" % b for b in range(c0 // 512, (c1 - 1) // 512 + 1)]

    S.op("pool", lambda e: e.memset(cu_ext[:], 0.0), writes=["cu_ext"])
    S.op("pool", lambda e: e.memset(gz_ext[:], 0.0), writes=["gz_ext"])

    acc_rot = [0]

    def next_bank():
        b = 2 + acc_rot[0] % 6
        acc_rot[0] += 1
        return b

    xtile_ctr = [0]
    xinfo = {}

    def emit_xload(g):
        for t in range(4):
            n = xtile_ctr[0]
            xtile_ctr[0] += 1
            row0 = 512 * g + 128 * t
            S.op("sp", lambda e, n=n, row0=row0: e.dma_start(out=xs[n % 4][:], in_=xh_d[row0:row0 + 128, :]),
                 writes=["xs%d" % (n % 4)], dma_group="xs%d" % (n % 4))
            xinfo[(g, t)] = n

    def emit_xnorm(g, tiles=(0, 1, 2, 3)):
        for t in tiles:
            n = xinfo[(g, t)]
            xb = xs[n % 4]
            xnb = xn[n % 4]
            sc = n % 16
            S.op("act", lambda e, xb=xb, xnb=xnb, sc=sc: e.activation(out=xnb[:], in_=xb[:], func=AF.Square,
                                                                      accum_out=stat[:, sc:sc + 1]),
                 reads=["xs%d" % (n % 4)], writes=["xn%d" % (n % 4), "ss%d" % sc])
            S.op("act", lambda e, sc=sc: e.activation(out=stat[:, 16 + sc:17 + sc], in_=stat[:, sc:sc + 1],
                                                      func=AF.Ln, scale=1.0 / DM, bias=EPS),
                 reads=["ss%d" % sc], writes=["sr%d" % sc])
            S.op("act", lambda e, sc=sc: e.activation(out=stat[:, 32 + sc:33 + sc], in_=stat[:, 16 + sc:17 + sc],
                                                      func=AF.Exp, scale=-0.5),
                 reads=["sr%d" % sc], writes=["si%d" % sc])
            S.op("dve", lambda e, xb=xb, xnb=xnb, sc=sc: e.scalar_tensor_tensor(
                out=xnb[:], in0=xb[:], scalar=stat[:, 32 + sc:33 + sc], in1=nwb[:], op0=ALU.mult, op1=ALU.mult),
                 reads=["xs%d" % (n % 4), "si%d" % sc, "nwb"], writes=["xn%d" % (n % 4)])

    def emit_xtrans(g):
        hb = hT[hpar[g]]
        for t in range(4):
            n = xinfo[(g, t)]
            xnb = xn[n % 4]
            tb = n % 2
            pt = psum[tb][:].bitcast(BF16)
            for kc in range(8):
                S.op("pe", lambda e, pt=pt, xnb=xnb, kc=kc: e.transpose(
                    out=pt[:, 128 * kc:128 * kc + 128], in_=xnb[:, 128 * kc:128 * kc + 128], identity=ident[:]),
                     reads=["xn%d" % (n % 4), "ident"], writes=["pst%d" % tb])
            eng = "act" if t % 2 == 0 else "dve"
            outv = hb[:, :, 128 * t:128 * t + 128]
            inv = pt.rearrange("p (k n) -> p k n", k=8)
            if eng == "act":
                f = lambda e, outv=outv, inv=inv: e.activation(out=outv, in_=inv, func=AF.Copy)
            else:
                f = lambda e, outv=outv, inv=inv: e.tensor_copy(out=outv, in_=inv)
            S.op(eng, f, reads=["pst%d" % tb], writes=["hT%d_t%d" % (hpar[g], t)])

    def hkeys(g, c0=0, c1=512):
        return ["hT%d_t%d" % (hpar[g], t) for t in range(c0 // 128, (c1 - 1) // 128 + 1)]

    def proj_fm(g, fc, c0=0, c1=512):
        b = next_bank()
        if g == "tail":
            hb, hk = hT_tail, ["hT_tail"]
        else:
            hb, hk = hT[hpar[g]], hkeys(g, c0, c1)
        n = c1 - c0
        for kc in range(8):
            S.op("pe", lambda e, b=b, hb=hb, kc=kc, fc=fc, c0=c0, c1=c1, n=n: e.matmul(
                psum[b][:, 0:n], Wbf[:, kc, 128 * fc:128 * fc + 128], hb[:, kc, c0:c1],
                start=(kc == 0), stop=(kc == 7)),
                 reads=wkeys(128 * fc, 128 * fc + 128) + hk, writes=["ps%d" % b])
        return b, psum[b][:, 0:n]

    cvn = [0]

    def conv_chunk(g, j, c0, c1, e0, store_cols):
        n = c1 - c0
        k = cvn[0] % 2
        cvn[0] += 1
        bu, pu = proj_fm(g, j, c0, c1)
        bgc, pgc = proj_fm(g, 8 + j, c0, c1)
        S.op("act", lambda e, k=k, pu=pu, n=n: e.activation(out=u_sb[k][:, 0:n], in_=pu, func=AF.Copy),
             reads=["ps%d" % bu], writes=["u_sb%d" % k])
        S.op("dve", lambda e, k=k, pgc=pgc, j=j, n=n, e0=e0: e.tensor_tensor(
            out=cu_ext[:, j, e0:e0 + n], in0=u_sb[k][:, 0:n], in1=pgc, op=ALU.mult),
             reads=["u_sb%d" % k, "ps%d" % bgc, "cu_ext"], writes=["cu%d" % j])
        if not store_cols:
            return
        bgb, pgb = proj_fm(g, 4 + j, c0, c1)
        bzc, pzc = proj_fm(g, 12 + j, c0, c1)
        S.op("act", lambda e, k=k, pzc=pzc, n=n: e.activation(out=sz_sb[k][:, 0:n], in_=pzc, func=AF.Silu),
             reads=["ps%d" % bzc], writes=["sz_sb%d" % k])
        S.op("dve", lambda e, k=k, pgb=pgb, j=j, n=n: e.tensor_tensor(
            out=gz_ext[:, j, 1:1 + n], in0=sz_sb[k][:, 0:n], in1=pgb, op=ALU.mult),
             reads=["sz_sb%d" % k, "ps%d" % bgb, "gz_ext"], writes=["gz%d" % j])

    def conv_out(j, n, ybuf, ycol0, ext_lo, ykey):
        k = cvn[0] % 2
        cvn[0] += 1
        w0 = cst[:, 8 + 3 * j:9 + 3 * j]
        w1 = cst[:, 9 + 3 * j:10 + 3 * j]
        w2 = cst[:, 10 + 3 * j:11 + 3 * j]
        bb = cst[:, 20 + j:21 + j]
        a = ext_lo
        S.op("pool", lambda e, k=k, j=j, a=a, n=n: e.tensor_scalar(
            out=tcv[k][:, 0:n], in0=cu_ext[:, j, a:a + n], scalar1=w1, scalar2=bb, op0=ALU.mult, op1=ALU.add),
             reads=["cu%d" % j, "cu_ext", "cst"], writes=["tcv%d" % k])
        S.op("dve", lambda e, k=k, j=j, a=a, n=n: e.scalar_tensor_tensor(
            out=tcv[k][:, 0:n], in0=cu_ext[:, j, a - 1:a - 1 + n], scalar=w0, in1=tcv[k][:, 0:n],
            op0=ALU.mult, op1=ALU.add),
             reads=["cu%d" % j, "tcv%d" % k], writes=["tcv%d" % k])
        S.op("dve", lambda e, k=k, j=j, a=a, n=n: e.scalar_tensor_tensor(
            out=tcv[k][:, 0:n], in0=cu_ext[:, j, a + 1:a + 1 + n], scalar=w2, in1=tcv[k][:, 0:n],
            op0=ALU.mult, op1=ALU.add),
             reads=["cu%d" % j, "tcv%d" % k], writes=["tcv%d" % k])
        S.op("pool", lambda e, k=k, j=j, a=a, n=n: e.tensor_tensor(
            out=ybuf[:, j, ycol0:ycol0 + n], in0=tcv[k][:, 0:n], in1=gz_ext[:, j, a - 1:a - 1 + n], op=ALU.mult),
             reads=["tcv%d" % k, "gz%d" % j, "gz_ext"], writes=[ykey + "_%d" % j])

    def conv_carry(j):
        S.op("pool", lambda e, j=j: e.tensor_copy(out=cu_ext[:, j, 0:2], in_=cu_ext[:, j, 512:514]),
             reads=["cu%d" % j], writes=["cu%d" % j])
        S.op("pool", lambda e, j=j: e.tensor_copy(out=gz_ext[:, j, 0:1], in_=gz_ext[:, j, 512:513]),
             reads=["gz%d" % j], writes=["gz%d" % j])

    qkn = [0]
    qk_pending = []

    def qk_flush():
        while qk_pending:
            qk_pending.pop(0)()

    def qk_chunk(g, fc, gain_ap, dst, dkey):
        k = qkn[0] % 2
        qkn[0] += 1
        b, pq = proj_fm(g, fc)
        S.op("act", lambda e, k=k, pq=pq: e.activation(out=qsq[k][:], in_=pq, func=AF.Square),
             reads=["ps%d" % b], writes=["qsq%d" % k])
        qk_flush()

        def stage2():
            b2 = next_bank()
            S.op("pe", lambda e, k=k, b2=b2: e.matmul(psum[b2][:, :], blk[:], qsq[k][:], start=True, stop=True),
                 reads=["qsq%d" % k, "blk"], writes=["ps%d" % b2])
            S.op("act", lambda e, k=k, b2=b2: e.activation(out=rr[k][:], in_=psum[b2][:, :], func=AF.Ln,
                                                           scale=1.0 / 64, bias=EPS),
                 reads=["ps%d" % b2], writes=["rr%d" % k])
            S.op("act", lambda e, k=k: e.activation(out=rr[k][:], in_=rr[k][:], func=AF.Exp, scale=-0.5),
                 reads=["rr%d" % k], writes=["rr%d" % k])
            S.op("dve", lambda e, k=k, pq=pq: e.scalar_tensor_tensor(
                out=dst, in0=pq, scalar=gain_ap, in1=rr[k][:], op0=ALU.mult, op1=ALU.mult),
                 reads=["ps%d" % b, "rr%d" % k, "gq", "cst"], writes=[dkey])
        qk_pending.append(stage2)

    FIRST_OWN = HALO // 512
    LAST_OWN = (HALO + OWN) // 512 - 1
    order = [0, 1, 11, 10] + list(range(2, 10))
    hpar = {g: i % 2 for i, g in enumerate(order)}

    def prefetch_pair0():
        S.op("sp", lambda e: e.dma_start(out=KT_A[:], in_=Kd[:, 0, :]),
             reads=KD_KEYS, writes=["KT_A"], dma_group="ktp0")
        S.op("sp", lambda e: e.dma_start(out=QT_A[:], in_=Qd[:, 0, :]),
             reads=QD_KEYS, writes=["QT_A"], dma_group="qtp0")
        tmap = {}
        n = 0
        for res, tiles in attn_sequences(0):
            nt = len(tiles)
            src = bass.AP(Vd_t, (res + tiles[0][0]) * 512, [[512, 128], [128 * 512, nt], [1, 128]])
            S.op("sp", lambda e, n=n, nt=nt, src=src: e.dma_start(out=Vt0[:, n:n + nt, :], in_=src),
                 reads=VD_KEYS, writes=["Vt0_t%d" % (n + i) for i in range(nt)], dma_group="vt0")
            for tl in tiles:
                tmap[(res, tl[0])] = n
                n += 1
        return 0, tmap

    emit_xload(order[0])
    emit_xnorm(order[0])
    emit_weights(0, 2)
    emit_xtrans(order[0])
    emit_xload(order[1])
    emit_xnorm(order[1])
    emit_weights(2, 8)
    pre_v = None
    for oi, g in enumerate(order):
        if oi + 1 < len(order) and oi > 0:
            emit_xtrans(order[oi + 1])
        spread = oi >= 4 and oi + 2 < len(order)
        if oi + 2 < len(order):
            emit_xload(order[oi + 2])
            if 0 < oi < 4:
                emit_xnorm(order[oi + 2])
        own = FIRST_OWN <= g <= LAST_OWN
        go = g - FIRST_OWN
        sl = oi % 2
        last = oi == len(order) - 1
        if g == LAST_OWN + 1:
            S.op("act", lambda e, g=g: e.activation(out=hT_tail[:, :, 0:1], in_=hT[hpar[g]][:, :, 0:1], func=AF.Copy),
                 reads=hkeys(g, 0, 1), writes=["hT_tail"])
        hb = hT[hpar[g]]
        for t in range(4):
            b = next_bank()
            for kc in range(8):
                S.op("pe", lambda e, b=b, hb=hb, kc=kc, t=t: e.matmul(
                    psum[b][:, :], hb[:, kc, 128 * t:128 * t + 128], Wbf[:, kc, 3072:3584],
                    start=(kc == 0), stop=(kc == 7)),
                     reads=wkeys(3072, 3584) + hkeys(g, 128 * t, 128 * t + 128), writes=["ps%d" % b])
            S.op("act", lambda e, b=b, t=t: e.activation(out=stV[0][:, t, :], in_=psum[b][:, :], func=AF.Copy),
                 reads=["ps%d" % b], writes=["stV0_%d" % t])
        S.op("sp", lambda e, g=g: e.dma_start(
            out=Vd[512 * g:512 * g + 512, :].rearrange("(t p) f -> p t f", p=128), in_=stV[0][:]),
             reads=["stV0_%d" % t for t in range(4)], writes=["Vd_g%d" % g] + ["stV0_%d" % t for t in range(4)],
             dma_group="stV0")
        for p in range(4):
            if own:
                qk_chunk(g, 16 + p, gq[:, 0:1], stQ[sl][:, p, :], "stQ%d_%d" % (sl, p))
        if own:
            qk_flush()
            S.op("sp", lambda e, sl=sl, go=go: e.dma_start(out=Qd[:, :, 512 * go:512 * go + 512], in_=stQ[sl][:]),
                 reads=["stQ%d_%d" % (sl, p) for p in range(4)],
                 writes=["Qd_g%d" % go] + ["stQ%d_%d" % (sl, p) for p in range(4)], dma_group="stQ%d" % sl)
        for p in range(4):
            qk_chunk(g, 20 + p, cst[:, 25:26], stK[sl][:, p, :], "stK%d_%d" % (sl, p))
        qk_flush()
        S.op("sp", lambda e, sl=sl, g=g: e.dma_start(out=Kd[:, :, 512 * g:512 * g + 512], in_=stK[sl][:]),
             reads=["stK%d_%d" % (sl, p) for p in range(4)],
             writes=["Kd_g%d" % g] + ["stK%d_%d" % (sl, p) for p in range(4)], dma_group="stK%d" % sl)
        if oi == 0:
            emit_xtrans(order[1])
            emit_xnorm(order[2])
        if last:
            pre_v = prefetch_pair0()
        if g == FIRST_OWN - 1:
            for j in range(4):
                conv_chunk(g, j, 511, 512, 1, False)
        if own:
            for j in range(4):
                conv_chunk(g, j, 0, 512, 2, True)
                conv_out(j, 512, stY[sl], 0, 1, "stY%d" % sl)
                conv_carry(j)
                if spread:
                    emit_xnorm(order[oi + 2], tiles=(j,))
            ykeys = ["stY%d_%d" % (sl, j) for j in range(4)]
            if go == 0:
                S.op("sp", lambda e, sl=sl: e.dma_start(out=Yd[:, :, 0:511], in_=stY[sl][:, :, 1:512]),
                     reads=ykeys, writes=["Yd"] + ykeys, dma_group="stY%d" % sl)
            else:
                S.op("sp", lambda e, sl=sl, go=go: e.dma_start(out=Yd[:, :, 512 * go - 1:512 * go + 511],
                                                              in_=stY[sl][:, :, :]),
                     reads=ykeys, writes=["Yd"] + ykeys, dma_group="stY%d" % sl)
        if g == LAST_OWN:
            so = 1 - sl
            for j in range(4):
                conv_chunk("tail", j, 0, 1, 2, False)
                conv_out(j, 1, stY[so], 0, 1, "stY%d" % so)
            ykeys = ["stY%d_%d" % (so, j) for j in range(4)]
            S.op("sp", lambda e, so=so: e.dma_start(out=Yd[:, :, OWN - 1:OWN], in_=stY[so][:, :, 0:1],
                                                    allow_slow_non_contiguous=True),
                 reads=ykeys, writes=["Yd"] + ykeys, dma_group="stY%d" % so)
        if own:
            for p in range(4):
                b, pz = proj_fm(g, 28 + p)
                S.op("act", lambda e, p=p, pz=pz: e.activation(out=stZ[0][:, p, :], in_=pz, func=AF.Silu),
                     reads=["ps%d" % b], writes=["stZ0_%d" % p])
            S.op("sp", lambda e, go=go: e.dma_start(out=Zd[:, :, 512 * go:512 * go + 512], in_=stZ[0][:]),
                 reads=["stZ0_%d" % p for p in range(4)], writes=["Zd"] + ["stZ0_%d" % p for p in range(4)],
                 dma_group="stZ0")

    S.barrier(lambda e: e.memset(dummy[:, 0:1], 0.0), "fenceAB")
    FB = ["fenceAB"]
    cur[0] = P0
    Yattn, _ = alloc("Yattn", [128, 4, OWN], BF16)
    Wobf, _ = alloc("Wobf", [128, 8, 1024], BF16)
    PC0 = cur[0]
    KT_B, _ = alloc("KT_B", [128, LOC], BF16)
    QT_B, _ = alloc("QT_B", [128, OWN], BF16)
    ZSp = [alloc("ZSp%d" % i, [128, OWN], BF16)[0] for i in range(2)]
    Vt1, _ = alloc("Vt1", [128, 48, 128], BF16)
    Vt = [Vt0, Vt1]
    expb, _ = alloc("expb", [128, 3, 8, 256], BF16)
    accb, _ = alloc("accb", [128, 2, OWN], F32)
    Eb = [alloc("E%d" % i, [128, 2, 512], BF16)[0] for i in range(3)]
    Pb = [alloc("P%d" % i, [128, 2, 512], BF16)[0] for i in range(3)]

    vones, _ = alloc("vones", [128, NM, 64], BF16)
    vb_b = bass.AP(vb, 0, [[NM, 128], [1, NM], [0, 64]])
    S.op("dve", lambda e: e.tensor_scalar(out=vones[:], in0=vb_b, scalar1=-1.0 / NEG, scalar2=1.0,
                                          op0=ALU.mult, op1=ALU.add),
         reads=["vb"] + FB, writes=["vones"])
    ycb_p, _ = alloc("ycb_p", [128, 4, 512], BF16)
    xr_p = [alloc("xr_p%d" % i, [128, 1024], F32)[0] for i in range(2)]
    assert cur[0] <= PF0, (cur[0], PF0)

    KTn, QTn = [KT_A, KT_B], [QT_A, QT_B]
    KTk, QTk = ["KT_A", "KT_B"], ["QT_A", "QT_B"]

    def load_pair(p, with_z=True):
        sl = p % 2
        S.op("sp", lambda e, p=p, sl=sl: e.dma_start(out=KTn[sl][:], in_=Kd[:, p, :]),
             reads=KD_KEYS + FB, writes=[KTk[sl]], dma_group="ktp%d" % sl)
        S.op("sp", lambda e, p=p, sl=sl: e.dma_start(out=QTn[sl][:], in_=Qd[:, p, :]),
             reads=QD_KEYS + FB, writes=[QTk[sl]], dma_group="qtp%d" % sl)
        if with_z:
            load_z(p)

    def load_z(p):
        S.op("sp", lambda e, p=p: e.dma_start(out=ZSp[p % 2][:], in_=Zd[:, p, :]),
             reads=["Zd"] + FB, writes=["ZSp%d" % (p % 2)], dma_group="zsp%d" % (p % 2))

    def deint_piece(res):
        S.op("act", lambda e, res=res: e.activation(out=KT_B[:, 384 * res:384 * res + 384],
                                                    in_=KT_A[:, res:res + 383 * 16 + 1:16], func=AF.Copy),
             reads=["KT_A"] + FB, writes=["KT_B"])
        S.op("act", lambda e, res=res: e.activation(out=QT_B[:, 256 * res:256 * res + 256],
                                                    in_=QT_A[:, res:res + 255 * 16 + 1:16], func=AF.Copy),
             reads=["QT_A"] + FB, writes=["QT_B"])

    vstep = [0]

    def load_vtiles(p, di):
        sl = vstep[0] % 2
        vstep[0] += 1
        d = DILS[di]
        tmap = {}
        n = 0
        for res, tiles in attn_sequences(di):
            nt = len(tiles)
            T00 = tiles[0][0]
            src = bass.AP(Vd_t, (res + d * T00) * 512 + 128 * p,
                          [[d * 512, 128], [128 * d * 512, nt], [1, 128]])
            S.op("sp", lambda e, sl=sl, n=n, nt=nt, src=src: e.dma_start(out=Vt[sl][:, n:n + nt, :], in_=src),
                 reads=VD_KEYS + FB, writes=["Vt%d_t%d" % (sl, n + i) for i in range(nt)], dma_group="vt%d" % sl)
            for (T0, useA, useB, masked) in tiles:
                tmap[(res, T0)] = n
                n += 1
        return sl, tmap

    sbank = {0: [2, 4], 1: [3, 5]}
    oslots = [0, 1, 6, 7]
    sctr = [0]
    blkctr = [0]
    mulctr = [0]
    evctr = [0]

    def attention_stages(p, di, vsl, tmap, first_dil):
        stages = []
        d = DILS[di]
        seqs = attn_sequences(di)
        if d == 16:
            chunks = []
            for r0 in range(0, 16, 2):
                for ti in range(3):
                    chunks.append([(seqs[r0][0], seqs[r0][1][ti]), (seqs[r0 + 1][0], seqs[r0 + 1][1][ti])])
            blkctr[0] = (blkctr[0] + 1) // 2 * 2
        else:
            flat = [(res, tl) for res, tiles in seqs for tl in tiles]
            chunks = [flat[i:i + 2] for i in range(0, len(flat), 2)]
        open_blocks = {}
        for ch in chunks:
            lay = []
            cpos = 0
            for i, (res, (T0, useA, useB, masked)) in enumerate(ch):
                q0 = T0 - 64 if useA else T0 + 64
                w = 128 * (int(useA) + int(useB))
                lay.append((cpos, q0, w, 0 if useA else 128))
                cpos += w
            width = cpos
            full = len(ch) == 2 and lay[0][2] == lay[1][2] and lay[0][3] == lay[1][3]

            def stage1(ch=ch, lay=lay, width=width, full=full):
                r2 = sctr[0] % 2
                r3 = sctr[0] % 3
                sctr[0] += 1
                b0 = sbank[0][r2]
                for i, (res, (T0, useA, useB, masked)) in enumerate(ch):
                    c0, q0, w, boff = lay[i]
                    for hh in range(2):
                        sb_ = sbank[hh][r2]
                        if d == 16 and False:
                            kap = KT_B[64 * hh:64 * hh + 64, 384 * res + T0:384 * res + T0 + 128]
                            qap = QT_B[64 * hh:64 * hh + 64, 256 * res + q0 - 64:256 * res + q0 - 64 + w]
                            rk = ["KT_B", "QT_B"]
                        else:
                            ks = res + d * T0
                            qs = res + d * q0 - HALO
                            kap = KTn[p % 2][64 * hh:64 * hh + 64, ks:ks + 127 * d + 1:d]
                            qap = QTn[p % 2][64 * hh:64 * hh + 64, qs:qs + (w - 1) * d + 1:d]
                            rk = [KTk[p % 2], QTk[p % 2]]
                        S.op("pe", lambda e, sb_=sb_, hh=hh, c0=c0, w=w, kap=kap, qap=qap: e.matmul(
                            psum[sb_][:, c0:c0 + w], kap, qap, start=True, stop=True,
                            tile_position=(64 * hh, 0)),
                             reads=rk + FB, writes=["ps%d" % sb_])
                Et, Pt = Eb[r3], Pb[r3]
                hb0 = (di * 8 + 2 * p) * 256
                for i in range(len(ch)):
                    c0, q0, w, boff = lay[i]
                    ei = bass.AP(psall, 512 * b0 + c0, [[4096, 128], [512, 2], [1, w]])
                    S.op("act", lambda e, Et=Et, ei=ei, c0=c0, w=w: e.activation(
                        out=Et[:, :, c0:c0 + w], in_=ei, func=AF.Exp),
                         reads=["ps%d" % b0, "ps%d" % (b0 + 1)] + FB, writes=["E%d_%d" % (r3, i)])
                    bin_ = bass.AP(expb, hb0 + boff, [[3 * 8 * 256, 128], [256, 2], [1, w]])
                    S.op("dve", lambda e, Et=Et, Pt=Pt, c0=c0, w=w, bin_=bin_: e.tensor_tensor(
                        out=Pt[:, :, c0:c0 + w], in0=Et[:, :, c0:c0 + w], in1=bin_, op=ALU.mult),
                         reads=["E%d_%d" % (r3, i), "expb%d" % di] + FB, writes=["P%d_%d" % (r3, i)])
                return r3

            plan = []
            for i, (res, (T0, useA, useB, masked)) in enumerate(ch):
                c0, q0, w, boff = lay[i]
                ts = tmap[(res, T0)]
                mi = MIDX[(di, res, T0)] if masked else None
                halves = []
                if useA:
                    halves.append((T0 - 64, c0))
                if useB:
                    halves.append((T0 + 64, c0 + (128 if useA else 0)))
                for (bstart, cc) in halves:
                    first = (res, bstart) not in open_blocks
                    if first:
                        open_blocks[(res, bstart)] = oslots[blkctr[0] % 4]
                        blkctr[0] += 1
                    ob = open_blocks[(res, bstart)]
                    if not first:
                        del open_blocks[(res, bstart)]
                    plan.append((res, ts, mi, bstart, cc, first, ob, i))

            def stage2(r, plan=plan):
                closed = []
                for (res, ts, mi, bstart, cc, first, ob, ti) in plan:
                    pso = psum[ob]
                    den_l = ones[:, 0:64] if mi is None else vones[:, mi, :]
                    Pt = Pb[r]
                    for hh in range(2):
                        S.op("pe", lambda e, pso=pso, hh=hh, ts=ts, cc=cc, Pt=Pt, first=first: e.matmul(
                            pso[64 * hh:64 * hh + 64, 0:128], Vt[vsl][:, ts, 64 * hh:64 * hh + 64],
                            Pt[:, hh, cc:cc + 128], start=first, stop=(not first), tile_position=(0, 64 * hh)),
                             reads=["P%d_%d" % (r, ti), "Vt%d_t%d" % (vsl, ts)] + FB, writes=["ps%d" % ob])
                    for hh in range(2):
                        S.op("pe", lambda e, pso=pso, hh=hh, cc=cc, Pt=Pt, first=first, den_l=den_l: e.matmul(
                            pso[64 * hh:64 * hh + 64, 128:256], den_l,
                            Pt[:, hh, cc:cc + 128], start=False, stop=(not first), skip_group_check=True,
                            tile_position=(0, 64 * hh)),
                             reads=["P%d_%d" % (r, ti), "ones", "vones"] + FB, writes=["ps%d" % ob])
                    if not first:
                        closed.append((res, bstart, ob))
                i = 0
                while i < len(closed):
                    res, bstart, ob = closed[i]
                    pair_ok = (d == 16 and i + 1 < len(closed) and closed[i + 1][0] == res + 1
                               and closed[i + 1][1] == bstart and closed[i + 1][2] == ob + 1)
                    ts0 = res + d * bstart - HALO
                    last_tok = ts0 + 127 * d + (1 if pair_ok else 0)
                    akeys = ["acc%d" % c for c in range(ts0 // 1024, last_tok // 1024 + 1)]
                    if pair_ok:
                        accv = bass.AP(accb, ts0, [[2 * OWN, 128], [1, 2], [OWN, 2], [d, 128]])
                        pv = bass.AP(psall, 512 * ob, [[4096, 128], [512, 2], [128, 2], [1, 128]])
                        rk = ["ps%d" % ob, "ps%d" % (ob + 1)]
                        i += 2
                    else:
                        accv = accb[:, :, ts0:ts0 + 127 * d + 1:d]
                        pv = psum[ob][:, 0:256].rearrange("p (t w) -> p t w", t=2)
                        rk = ["ps%d" % ob]
                        i += 1
                    if first_dil:
                        evctr[0] += 1
                        if evctr[0] % 8 in (0, 3, 6):
                            S.op("act", lambda e, accv=accv, pv=pv: e.activation(out=accv, in_=pv, func=AF.Copy),
                                 reads=rk + FB, writes=akeys)
                        else:
                            S.op("dve", lambda e, accv=accv, pv=pv: e.tensor_copy(out=accv, in_=pv),
                                 reads=rk + FB, writes=akeys)
                    else:
                        S.op("dve", lambda e, accv=accv, pv=pv: e.tensor_tensor(
                            out=accv, in0=accv, in1=pv, op=ALU.add),
                             reads=rk + akeys + FB, writes=akeys)
            stages.append((stage1, stage2))
        assert not open_blocks
        return stages

    def normalise(p, c):
        psl = p % 2
        cs = slice(1024 * c, 1024 * c + 1024)
        ak = ["acc%d" % c]
        S.op("act", lambda e: e.activation(out=accb[:, 1, cs], in_=accb[:, 1, cs], func=AF.Ln),
             reads=ak + FB, writes=ak)
        S.op("act", lambda e: e.activation(out=accb[:, 1, cs], in_=accb[:, 1, cs], func=AF.Exp, scale=-1.0),
             reads=ak + FB, writes=ak)
        S.op("dve", lambda e: e.tensor_tensor(out=accb[:, 0, cs], in0=accb[:, 0, cs],
                                              in1=accb[:, 1, cs], op=ALU.mult),
             reads=ak + FB, writes=ak)
        S.op("dve", lambda e: e.tensor_tensor(
            out=Yattn[:, p, cs], in0=accb[:, 0, cs], in1=ZSp[psl][:, cs], op=ALU.mult),
             reads=ak + ["ZSp%d" % psl] + FB, writes=["Yattn_p%d_c%d" % (p, c)])

    bias_exp_later = []

    def load_bias_tables(dis, defer_exp=False):
        stg = [accb[:, 0, 0:2048], accb[:, 0, 2048:4096], accb[:, 1, 0:2048]]
        sk = [["acc0", "acc1"], ["acc2", "acc3"], ["acc0", "acc1"]]
        for di in dis:
            v = stg[di].rearrange("p (h q) -> p h q", h=8)
            S.op("sp", lambda e, di=di, v=v: e.dma_start(out=v, in_=bias_d[di]),
                 reads=FB, writes=["bst%d" % di], dma_group="bst%d" % di)

            def do_exp(di=di, v=v):
                S.op("act", lambda e: e.activation(out=expb[:, di, :, :], in_=v, func=AF.Exp),
                     reads=["bst%d" % di] + FB, writes=["expb%d" % di] + sk[di])
            if defer_exp:
                bias_exp_later.append(do_exp)
            else:
                do_exp()

    wout_v = wout_d.rearrange("(kc p) n -> p kc n", p=128)

    def load_wout():
        for hf in range(2):
            S.op("pool", lambda e, hf=hf: e.dma_start(out=Wobf[:, :, 512 * hf:512 * hf + 512],
                                                      in_=wout_v[:, :, 512 * hf:512 * hf + 512]),
                 reads=FB, writes=["Wobf%d" % hf], dma_group="wold%d" % hf)

    steps = [(p, di) for p in range(4) for di in range(3)]
    vinfo = {0: pre_v}
    vstep[0] = 1
    load_bias_tables([0])
    load_bias_tables([1, 2], defer_exp=True)
    load_z(0)
    load_wout()
    pend = None
    norm_todo = []
    for sidx, (p, di) in enumerate(steps):
        if di == 0:
            deint = list(range(16))
        vsl, tmap = vinfo.pop(sidx)
        for si, (st1, st2) in enumerate(attention_stages(p, di, vsl, tmap, di == 0)):
            r = st1()
            while bias_exp_later:
                bias_exp_later.pop(0)()
            if pend is not None:
                pend[0](pend[1])
            pend = (st2, r)
            if si == 1 and di == 0 and p + 1 < 4:
                load_pair(p + 1, with_z=False)
            if si == 6 and di == 0 and p + 1 < 4:
                load_z(p + 1)
            if si == 1 and sidx + 1 < len(steps):
                vinfo[sidx + 1] = load_vtiles(*steps[sidx + 1])
            if norm_todo:
                normalise(*norm_todo.pop(0))
            if False and di < 2 and deint and si % 2 == 1:
                deint_piece(deint.pop(0))
        if di == 2:
            norm_todo = [(p, c) for c in range(4)]
    if pend is not None:
        pend[0](pend[1])
    for pc in norm_todo:
        normalise(*pc)

    NXR, NOB = 6, 4

    def c_loads(tt, fence):
        gg = tt // 4
        fk = fence
        if tt % 4 == 0:
            S.op("sp", lambda e, gg=gg: e.dma_start(out=ycb[gg % 2][:], in_=Yd[:, :, 512 * gg:512 * gg + 512]),
                 reads=["Yd"] + fk, writes=["ycb%d" % (gg % 2)], dma_group="ycb%d" % (gg % 2))
        S.op("sp", lambda e, tt=tt: e.dma_start(out=xr[tt % NXR][:],
                                                in_=xh_d[HALO + 128 * tt:HALO + 128 * tt + 128, :]),
             reads=fk, writes=["xr%d" % (tt % NXR)], dma_group="xr%d" % (tt % NXR))

    ycb = [ycb_p, None]
    xr = xr_p + [None] * (NXR - 2)
    for tt in range(2):
        c_loads(tt, [])
    S.barrier(lambda e: e.memset(dummy[:, 1:2], 0.0), "fenceBC")
    FC = ["fenceBC"]
    cur[0] = PC0
    ycb[1], _ = alloc("ycb1", [128, 4, 512], BF16)
    for i in range(2, NXR):
        xr[i], _ = alloc("xr%d" % i, [128, 1024], F32)
    ob = [alloc("ob%d" % i, [128, 1024], F32)[0] for i in range(NOB)]
    cbank = [0]

    PRE = 4
    c_loads(2, FC)
    c_loads(3, FC)
    for tt in range(32):
        gg = tt // 4
        if tt + PRE < 32:
            c_loads(tt + PRE, FC)
        mm_f = [] if gg == 0 else FC
        for hf in range(2):
            b = cbank[0] % 8
            cbank[0] += 1
            for ch in range(8):
                if ch < 4:
                    lhsT = ycb[gg % 2][:, ch, 128 * (tt % 4):128 * (tt % 4) + 128]
                    rk = ["ycb%d" % (gg % 2)]
                else:
                    lhsT = Yattn[:, ch - 4, 128 * tt:128 * tt + 128]
                    rk = ["Yattn_p%d_c%d" % (ch - 4, tt // 8)]
                S.op("pe", lambda e, b=b, lhsT=lhsT, ch=ch, hf=hf: e.matmul(
                    psum[b][:, :], lhsT, Wobf[:, ch, 512 * hf:512 * hf + 512], start=(ch == 0), stop=(ch == 7)),
                     reads=rk + ["Wobf%d" % hf] + mm_f,
                     writes=["ps%d" % b])
            S.op("dve", lambda e, b=b, tt=tt, hf=hf: e.tensor_tensor(
                out=ob[tt % NOB][:, 512 * hf:512 * hf + 512], in0=xr[tt % NXR][:, 512 * hf:512 * hf + 512],
                in1=psum[b][:, :], op=ALU.add),
                 reads=["ps%d" % b, "xr%d" % (tt % NXR)] + FC, writes=["ob%d_%d" % (tt % NOB, hf)])
        S.op("act", lambda e, tt=tt: e.dma_start(out=out_d[128 * tt:128 * tt + 128, :], in_=ob[tt % NOB][:]),
             reads=["ob%d_0" % (tt % NOB), "ob%d_1" % (tt % NOB)] + FC,
             writes=["ob%d_0" % (tt % NOB), "ob%d_1" % (tt % NOB)], dma_group="ob%d" % (tt % NOB))

    S.emit(nc, final_groups=["ob%d" % i for i in range(NOB)])
    return nc


def host_tables(rel_bias):
    a = np.arange(128)[:, None]
    b = np.arange(256)[None, :]
    rel = a - b + 64
    band = np.abs(rel) <= 64
    out = np.full((3, 128, 8, 256), NEG, np.float32)
    for di, d in enumerate(DILS):
        buckets = t5_bucket_np(np.clip(rel, -64, 64) * d)
        g = rel_bias[buckets]
        g = np.transpose(g, (0, 2, 1))
        m = np.broadcast_to(band[:, None, :], g.shape)
        out[di][m] = g[m]
    return out


_CACHE = {}


def kernel(x, norm_w, w_in, conv_w, conv_b, q_norm_w, k_norm_w, rel_bias, w_out):
    x = np.asarray(x, np.float32)
    w_in = np.ascontiguousarray(np.asarray(w_in, np.float32))
    w_out = np.ascontiguousarray(np.asarray(w_out, np.float32))
    norm_w = np.asarray(norm_w, np.float32)
    conv_w = np.asarray(conv_w, np.float32)
    conv_b = np.asarray(conv_b, np.float32)
    q_norm_w = np.asarray(q_norm_w, np.float32)
    k_norm_w = np.asarray(k_norm_w, np.float32)
    rel_bias = np.asarray(rel_bias, np.float32)

    MIDX, NM = masked_tile_index()
    cst = np.zeros((128, 32), np.float32)
    cst[:, 0:8] = norm_w.reshape(8, 128).T
    for j in range(4):
        for k in range(3):
            cst[:, 8 + 3 * j + k] = conv_w[k, 128 * j:128 * j + 128]
        cst[:, 20 + j] = conv_b[128 * j:128 * j + 128]
    cst[:, 24] = np.tile(q_norm_w, 2)
    cst[:, 25] = np.tile(k_norm_w, 2)
    biasT = host_tables(rel_bias)
    nwb = np.ascontiguousarray(np.broadcast_to(norm_w[None, :], (128, DM)))

    in_maps = []
    for c in range(NCORES):
        b = c // 4
        s0 = (c % 4) * OWN
        xh = np.zeros((LOC, DM), np.float32)
        lo = max(0, s0 - HALO)
        hi = min(SEQ, s0 + OWN + HALO)
        xh[lo - (s0 - HALO):hi - (s0 - HALO)] = x[b, lo:hi]
        vbm = np.zeros((128, NM), np.float32)
        for (di, res, T0), mi in MIDX.items():
            tl = res + DILS[di] * (T0 + np.arange(128))
            pos = s0 - HALO + tl
            vbm[:, mi] = np.where((pos >= 0) & (pos < SEQ), 0.0, NEG)
        in_maps.append({"xh": xh, "w_in": w_in, "w_out": w_out, "cst": cst, "vb": vbm, "biasT": biasT,
                        "nwb": nwb})

    if "nc" not in _CACHE:
        _CACHE["nc"] = build_program()
    nc = _CACHE["nc"]
    res = run_bass_kernel_spmd(nc, in_maps, core_ids=list(range(NCORES)))
    out = np.empty((2, SEQ, DM), np.float32)
    for c in range(NCORES):
        b = c // 4
        s0 = (c % 4) * OWN
        out[b, s0:s0 + OWN] = res.results[c]["out"]
    return out
```

```python
import contextlib
import math
import numpy as np
import concourse.bass as bass
import concourse.mybir as mybir
from concourse.bass_utils import run_bass_kernel_spmd

F32 = mybir.dt.float32
BF16 = mybir.dt.bfloat16
AF = mybir.ActivationFunctionType
ALU = mybir.AluOpType

NCORES = 8
SEQ = 16384
DM = 1024
OWN = 4096
HALO = 1024
LOC = OWN + 2 * HALO
NG = LOC // 512
NEG = -30000.0
EPS = 1e-6
DILS = (1, 4, 16)


class Sched:
    EPOCH = 12000

    def __init__(self):
        self.ops = []
        self.last_writer = {}
        self.readers = {}
        self.last_on_eng = {}
        self.last_in_group = {}

    def op(self, eng, fn, reads=(), writes=(), dma_group=None, extra_deps=()):
        idx = len(self.ops)
        deps = set(extra_deps)
        for k in reads:
            w = self.last_writer.get(k)
            if w is not None:
                deps.add(w)
        for k in writes:
            w = self.last_writer.get(k)
            if w is not None:
                deps.add(w)
            deps.update(self.readers.get(k, ()))
        deps.discard(idx)
        for k in reads:
            self.readers.setdefault(k, []).append(idx)
        for k in writes:
            self.last_writer[k] = idx
            self.readers[k] = []
        self.ops.append(dict(eng=eng, fn=fn, deps=deps, dma_group=dma_group, signal=None))
        if dma_group is None:
            self.last_on_eng[eng] = idx
        else:
            self.last_in_group[dma_group] = idx
        return idx

    def barrier(self, fn, key):
        deps = set(self.last_on_eng.values()) | set(self.last_in_group.values())
        return self.op("pool", fn, writes=[key], extra_deps=deps)

    @staticmethod
    def _skip(od, o):
        return (od["eng"] == "pe" and o["eng"] == "pe" and od["dma_group"] is None
                and o["dma_group"] is None)

    def emit(self, nc, final_groups=()):
        ops = self.ops
        needed = set()
        for o in ops:
            for d in o["deps"]:
                if not self._skip(ops[d], o):
                    needed.add(d)
        counters = {}
        ecount = {}
        group_ops = {}
        for i, o in enumerate(ops):
            if o["dma_group"] is not None:
                key = "d_" + o["dma_group"]
                counters[key] = counters.get(key, 0) + 16
                o["signal"] = (key, counters[key], 16)
                group_ops.setdefault(o["dma_group"], []).append(i)
            elif i in needed:
                n = ecount.get(o["eng"], 0)
                ecount[o["eng"]] = n + 1
                key = "e_%s_%d" % (o["eng"], n // self.EPOCH)
                counters[key] = counters.get(key, 0) + 1
                o["signal"] = (key, counters[key], 1)
        keys = sorted(counters.keys())
        with contextlib.ExitStack() as es:
            sems = {k: es.enter_context(nc.semaphore(k)) for k in keys}
            streams = {}
            for i, o in enumerate(ops):
                streams.setdefault(o["eng"], []).append(i)
            plans = {}
            import bisect
            for e, lst in streams.items():
                waited = {}
                plan = []
                for i in lst:
                    o = ops[i]
                    w = {}
                    for d in o["deps"]:
                        od = ops[d]
                        if od["signal"] is None or self._skip(od, o):
                            continue
                        if od["dma_group"] is not None:
                            gl = group_ops[od["dma_group"]]
                            j = bisect.bisect_left(gl, i) - 1
                            od = ops[gl[j]]
                        k, c, _ = od["signal"]
                        if waited.get(k, 0) >= c:
                            continue
                        w[k] = max(w.get(k, 0), c)
                    for k, c in w.items():
                        waited[k] = c
                    plan.append((i, sorted(w.items())))
                plans[e] = plan
            final = [("d_" + g, counters["d_" + g]) for g in final_groups]
            engmap = {"pe": "tensor", "act": "scalar", "dve": "vector", "pool": "gpsimd",
                      "sp": "sync"}
            with nc.Block() as block:
                for e in ["sp", "pe", "act", "dve", "pool"]:
                    plan = plans.get(e, [])

                    def body(engine, plan=plan, e=e):
                        for i, ws in plan:
                            for k, c in ws:
                                engine.wait_ge(sems[k], c)
                            o = ops[i]
                            ins = o["fn"](engine)
                            if o["signal"] is not None:
                                k, c, inc = o["signal"]
                                ins.then_inc(sems[k], inc)
                        if e == "sp":
                            for k, c in final:
                                engine.wait_ge(sems[k], c)
                    getattr(block, engmap[e])(body)
        return counters


def attn_sequences(di):
    d = DILS[di]
    seqs = []
    if d == 16:
        t0s = [0, 128, 256]
        own_lo, own_hi = 64, 320
    else:
        own_lo, own_hi = HALO // d, (HALO + OWN) // d
        t0s = [128 * j + 64 for j in range(own_lo // 128 - 1, own_hi // 128)]
    for res in range(d):
        tiles = []
        for T0 in t0s:
            useA = own_lo <= T0 - 64 and T0 + 64 <= own_hi
            useB = own_lo <= T0 + 64 and T0 + 192 <= own_hi
            lo_tok = res + d * T0
            hi_tok = res + d * (T0 + 127)
            masked = lo_tok < HALO or hi_tok >= HALO + OWN
            tiles.append((T0, useA, useB, masked))
        seqs.append((res, tiles))
    return seqs


def masked_tile_index():
    idx = {}
    n = 0
    for di in range(3):
        for res, tiles in attn_sequences(di):
            for (T0, useA, useB, masked) in tiles:
                if masked:
                    idx[(di, res, T0)] = n
                    n += 1
    return idx, n


def t5_bucket_np(rel):
    nb = 32
    half_b = nb // 2
    max_exact = half_b // 2
    ret = np.where(rel > 0, half_b, 0)
    n = np.abs(rel)
    nf = np.maximum(n, 1).astype(np.float32)
    large = max_exact + (np.log(nf / np.float32(max_exact)) / np.float32(math.log(1024 / max_exact))
                         * (half_b - max_exact)).astype(np.int32)
    large = np.minimum(large, half_b - 1)
    return ret + np.where(n < max_exact, n, large)


def build_program():
    nc = bass.Bass("TRN2", target_bir_lowering=False)
    MIDX, NM = masked_tile_index()

    xh_d = nc.dram_tensor("xh", [LOC, DM], F32, kind="ExternalInput").ap()
    win_d = nc.dram_tensor("w_in", [DM, 4096], F32, kind="ExternalInput").ap()
    wout_d = nc.dram_tensor("w_out", [DM, DM], F32, kind="ExternalInput").ap()
    cst_d = nc.dram_tensor("cst", [128, 32], F32, kind="ExternalInput").ap()
    nwb_d = nc.dram_tensor("nwb", [128, DM], F32, kind="ExternalInput").ap()
    vb_d = nc.dram_tensor("vb", [128, NM], F32, kind="ExternalInput").ap()
    bias_d = nc.dram_tensor("biasT", [3, 128, 8, 256], F32, kind="ExternalInput").ap()
    out_d = nc.dram_tensor("out", [OWN, DM], F32, kind="ExternalOutput").ap()
    Qd = nc.dram_tensor("Qd", [128, 4, OWN], BF16).ap()
    Kd = nc.dram_tensor("Kd", [128, 4, LOC], BF16).ap()
    Zd = nc.dram_tensor("Zd", [128, 4, OWN], BF16).ap()
    Yd = nc.dram_tensor("Yd", [128, 4, OWN], BF16).ap()
    Vd_t = nc.dram_tensor("Vd", [LOC, 512], BF16)
    Vd = Vd_t.ap()

    S = Sched()
    base = (nc.sbuf_base + 31) // 32 * 32
    top = nc.sbuf_top
    cur = [base]

    def alloc(name, shape, dt, at=None):
        nbytes = int(np.prod(shape[1:])) * (4 if dt == F32 else 2)
        nbytes = (nbytes + 31) // 32 * 32
        if at is None:
            off = cur[0]
            cur[0] += nbytes
        else:
            off = at
        assert off + nbytes <= top, (name, off, nbytes, top)
        return nc.alloc_sbuf_tensor_at(name, list(shape), dt, offset=off), off + nbytes

    ident, _ = alloc("ident", [128, 128], BF16)
    ones, _ = alloc("ones", [128, 128], BF16)
    blk, _ = alloc("blk", [128, 128], BF16)
    cst, _ = alloc("cst_sb", [128, 32], F32)
    vb, _ = alloc("vb_sb", [128, NM], F32)
    stat, _ = alloc("stat", [128, 64], F32)
    gq, _ = alloc("gq", [128, 2], F32)
    dummy, _ = alloc("dummy", [128, 8], F32)
    nwb, _ = alloc("nwb_sb", [128, DM], F32)
    P0 = cur[0]

    psall = nc.alloc_psum_tensor("psall", [128, 4096], F32)
    psum = [psall[:, 512 * i:512 * i + 512] for i in range(8)]

    S.op("sp", lambda e: e.dma_start(out=cst[:], in_=cst_d[:, :]), writes=["cst"], dma_group="cst")
    S.op("sp", lambda e: e.dma_start(out=vb[:], in_=vb_d[:, :]), writes=["vb"], dma_group="vbl")
    S.op("sp", lambda e: e.dma_start(out=nwb[:], in_=nwb_d[:, :]), writes=["nwb"], dma_group="nwbl")
    S.op("pool", lambda e: e.memset(ident[:], 1.0), writes=["ident"])
    S.op("pool", lambda e: e.affine_select(out=ident[:], in_=ident[:], pattern=[[-1, 128]],
                                           compare_op=ALU.is_equal, fill=0.0, base=0,
                                           channel_multiplier=1),
         reads=["ident"], writes=["ident"])
    S.op("pool", lambda e: e.memset(ones[:], 1.0), writes=["ones"])
    S.op("pool", lambda e: e.memset(blk[:], 0.0), writes=["blk"])
    S.op("pool", lambda e: e.memset(blk[0:64, 0:64], 1.0), reads=["blk"], writes=["blk"])
    S.op("pool", lambda e: e.memset(blk[64:128, 64:128], 1.0), reads=["blk"], writes=["blk"])
    S.op("dve", lambda e: e.tensor_scalar(out=gq[:, 0:1], in0=cst[:, 24:25], scalar1=0.125,
                                          scalar2=None, op0=ALU.mult),
         reads=["cst"], writes=["gq"])

    cur[0] = P0
    Wbf, _ = alloc("Wbf", [128, 8, 4096], BF16)
    xs = [alloc("xs%d" % i, [128, 1024], F32)[0] for i in range(4)]
    xn = [alloc("xn%d" % i, [128, 1024], BF16)[0] for i in range(4)]
    hT = [alloc("hT%d" % i, [128, 8, 512], BF16)[0] for i in range(2)]
    cu_ext, _ = alloc("cu_ext", [128, 4, 514], F32)
    gz_ext, _ = alloc("gz_ext", [128, 4, 513], F32)
    u_sb = [alloc("u_sb%d" % i, [128, 512], F32)[0] for i in range(2)]
    sz_sb = [alloc("sz_sb%d" % i, [128, 512], F32)[0] for i in range(2)]
    tcv = [alloc("tcv%d" % i, [128, 512], F32)[0] for i in range(2)]
    qsq = [alloc("qsq%d" % i, [128, 512], BF16)[0] for i in range(2)]
    rr = [alloc("rr%d" % i, [128, 512], F32)[0] for i in range(2)]
    stQ = [alloc("stQ%d" % i, [128, 4, 512], BF16)[0] for i in range(2)]
    stK = [alloc("stK%d" % i, [128, 4, 512], BF16)[0] for i in range(2)]
    stZ = [alloc("stZ%d" % i, [128, 4, 512], BF16)[0] for i in range(1)]
    stY = [alloc("stY%d" % i, [128, 4, 512], BF16)[0] for i in range(2)]
    stV = [alloc("stV%d" % i, [128, 4, 512], BF16)[0] for i in range(1)]
    hT_tail, _ = alloc("hT_tail", [128, 8, 16], BF16)
    PA_END = cur[0]
    PF0 = (top - (LOC + OWN + 48 * 128) * 2) // 32 * 32
    assert PA_END <= PF0, (PA_END, PF0)
    KT_A, pf1 = alloc("KT_A", [128, LOC], BF16, at=PF0)
    QT_A, pf2 = alloc("QT_A", [128, OWN], BF16, at=pf1)
    Vt0, pf3 = alloc("Vt0", [128, 48, 128], BF16, at=pf2)
    KD_KEYS = ["Kd_g%d" % g for g in range(NG)]
    QD_KEYS = ["Qd_g%d" % g for g in range(OWN // 512)]
    VD_KEYS = ["Vd_g%d" % g for g in range(NG)]

    win_v = win_d.rearrange("(kc p) n -> p kc n", p=128)
    wb_order = [6, 5, 4, 0, 2, 1, 3, 7]

    def emit_weights(lo, hi):
        for bi in range(lo, hi):
            b = wb_order[bi]
            S.op("pool", lambda e, b=b: e.dma_start(out=Wbf[:, :, 512 * b:512 * b + 512],
                                                    in_=win_v[:, :, 512 * b:512 * b + 512]),
                 writes=["Wbf_c%d" % b], dma_group="wld%d" % b)

    def wkeys(c0, c1):
        return ["Wbf_c%d" % b for b in range(c0 // 512, (c1 - 1) // 512 + 1)]


    acc_rot = [0]

    def next_bank():
        b = 2 + acc_rot[0] % 6
        acc_rot[0] += 1
        return b

    xtile_ctr = [0]
    xinfo = {}

    def emit_xload(g):
        for t in range(4):
            n = xtile_ctr[0]
            xtile_ctr[0] += 1
            row0 = 512 * g + 128 * t
            S.op("sp", lambda e, n=n, row0=row0: e.dma_start(out=xs[n % 4][:], in_=xh_d[row0:row0 + 128, :]),
                 writes=["xs%d" % (n % 4)], dma_group="xs%d" % (n % 4))
            xinfo[(g, t)] = n

    def emit_xnorm(g, tiles=(0, 1, 2, 3)):
        for t in tiles:
            n = xinfo[(g, t)]
            xb = xs[n % 4]
            xnb = xn[n % 4]
            sc = n % 16
            S.op("act", lambda e, xb=xb, xnb=xnb, sc=sc: e.activation(out=xnb[:], in_=xb[:], func=AF.Square,
                                                                      accum_out=stat[:, sc:sc + 1]),
                 reads=["xs%d" % (n % 4)], writes=["xn%d" % (n % 4), "ss%d" % sc])
            S.op("act", lambda e, sc=sc: e.activation(out=stat[:, 16 + sc:17 + sc], in_=stat[:, sc:sc + 1],
                                                      func=AF.Ln, scale=1.0 / DM, bias=EPS),
                 reads=["ss%d" % sc], writes=["sr%d" % sc])
            S.op("act", lambda e, sc=sc: e.activation(out=stat[:, 32 + sc:33 + sc], in_=stat[:, 16 + sc:17 + sc],
                                                      func=AF.Exp, scale=-0.5),
                 reads=["sr%d" % sc], writes=["si%d" % sc])
            S.op("dve", lambda e, xb=xb, xnb=xnb, sc=sc: e.scalar_tensor_tensor(
                out=xnb[:], in0=xb[:], scalar=stat[:, 32 + sc:33 + sc], in1=nwb[:], op0=ALU.mult, op1=ALU.mult),
                 reads=["xs%d" % (n % 4), "si%d" % sc, "nwb"], writes=["xn%d" % (n % 4)])

    def emit_xtrans(g):
        hb = hT[hpar[g]]
        for t in range(4):
            n = xinfo[(g, t)]
            xnb = xn[n % 4]
            tb = n % 2
            pt = psum[tb][:].bitcast(BF16)
            for kc in range(8):
                S.op("pe", lambda e, pt=pt, xnb=xnb, kc=kc: e.transpose(
                    out=pt[:, 128 * kc:128 * kc + 128], in_=xnb[:, 128 * kc:128 * kc + 128], identity=ident[:]),
                     reads=["xn%d" % (n % 4), "ident"], writes=["pst%d" % tb])
            eng = "act" if t % 2 == 0 else "dve"
            outv = hb[:, :, 128 * t:128 * t + 128]
            inv = pt.rearrange("p (k n) -> p k n", k=8)
            if eng == "act":
                f = lambda e, outv=outv, inv=inv: e.activation(out=outv, in_=inv, func=AF.Copy)
            else:
                f = lambda e, outv=outv, inv=inv: e.tensor_copy(out=outv, in_=inv)
            S.op(eng, f, reads=["pst%d" % tb], writes=["hT%d_t%d" % (hpar[g], t)])

    def hkeys(g, c0=0, c1=512):
        return ["hT%d_t%d" % (hpar[g], t) for t in range(c0 // 128, (c1 - 1) // 128 + 1)]

    def proj_fm(g, fc, c0=0, c1=512):
        b = next_bank()
        if g == "tail":
            hb, hk = hT_tail, ["hT_tail"]
        else:
            hb, hk = hT[hpar[g]], hkeys(g, c0, c1)
        n = c1 - c0
        for kc in range(8):
            S.op("pe", lambda e, b=b, hb=hb, kc=kc, fc=fc, c0=c0, c1=c1, n=n: e.matmul(
                psum[b][:, 0:n], Wbf[:, kc, 128 * fc:128 * fc + 128], hb[:, kc, c0:c1],
                start=(kc == 0), stop=(kc == 7)),
                 reads=wkeys(128 * fc, 128 * fc + 128) + hk, writes=["ps%d" % b])
        return b, psum[b][:, 0:n]

    cvn = [0]

    def conv_chunk(g, j, c0, c1, e0, store_cols):
        n = c1 - c0
        k = cvn[0] % 2
        cvn[0] += 1
        bu, pu = proj_fm(g, j, c0, c1)
        bgc, pgc = proj_fm(g, 8 + j, c0, c1)
        S.op("act", lambda e, k=k, pu=pu, n=n: e.activation(out=u_sb[k][:, 0:n], in_=pu, func=AF.Copy),
             reads=["ps%d" % bu], writes=["u_sb%d" % k])
        S.op("dve", lambda e, k=k, pgc=pgc, j=j, n=n, e0=e0: e.tensor_tensor(
            out=cu_ext[:, j, e0:e0 + n], in0=u_sb[k][:, 0:n], in1=pgc, op=ALU.mult),
             reads=["u_sb%d" % k, "ps%d" % bgc, "cu_ext"], writes=["cu%d" % j])
        if not store_cols:
            return
        bgb, pgb = proj_fm(g, 4 + j, c0, c1)
        bzc, pzc = proj_fm(g, 12 + j, c0, c1)
        S.op("act", lambda e, k=k, pzc=pzc, n=n: e.activation(out=sz_sb[k][:, 0:n], in_=pzc, func=AF.Silu),
             reads=["ps%d" % bzc], writes=["sz_sb%d" % k])
        S.op("dve", lambda e, k=k, pgb=pgb, j=j, n=n: e.tensor_tensor(
            out=gz_ext[:, j, 1:1 + n], in0=sz_sb[k][:, 0:n], in1=pgb, op=ALU.mult),
             reads=["sz_sb%d" % k, "ps%d" % bgb, "gz_ext"], writes=["gz%d" % j])

    def conv_out(j, n, ybuf, ycol0, ext_lo, ykey):
        k = cvn[0] % 2
        cvn[0] += 1
        w0 = cst[:, 8 + 3 * j:9 + 3 * j]
        w1 = cst[:, 9 + 3 * j:10 + 3 * j]
        w2 = cst[:, 10 + 3 * j:11 + 3 * j]
        bb = cst[:, 20 + j:21 + j]
        a = ext_lo
        S.op("pool", lambda e, k=k, j=j, a=a, n=n: e.tensor_scalar(
            out=tcv[k][:, 0:n], in0=cu_ext[:, j, a:a + n], scalar1=w1, scalar2=bb, op0=ALU.mult, op1=ALU.add),
             reads=["cu%d" % j, "cu_ext", "cst"], writes=["tcv%d" % k])
        S.op("dve", lambda e, k=k, j=j, a=a, n=n: e.scalar_tensor_tensor(
            out=tcv[k][:, 0:n], in0=cu_ext[:, j, a - 1:a - 1 + n], scalar=w0, in1=tcv[k][:, 0:n],
            op0=ALU.mult, op1=ALU.add),
             reads=["cu%d" % j, "tcv%d" % k], writes=["tcv%d" % k])
        S.op("dve", lambda e, k=k, j=j, a=a, n=n: e.scalar_tensor_tensor(
            out=tcv[k][:, 0:n], in0=cu_ext[:, j, a + 1:a + 1 + n], scalar=w2, in1=tcv[k][:, 0:n],
            op0=ALU.mult, op1=ALU.add),
             reads=["cu%d" % j, "tcv%d" % k], writes=["tcv%d" % k])
        S.op("pool", lambda e, k=k, j=j, a=a, n=n: e.tensor_tensor(
            out=ybuf[:, j, ycol0:ycol0 + n], in0=tcv[k][:, 0:n], in1=gz_ext[:, j, a - 1:a - 1 + n], op=ALU.mult),
             reads=["tcv%d" % k, "gz%d" % j, "gz_ext"], writes=[ykey + "_%d" % j])

    def conv_carry(j):
        S.op("pool", lambda e, j=j: e.tensor_copy(out=cu_ext[:, j, 0:2], in_=cu_ext[:, j, 512:514]),
             reads=["cu%d" % j], writes=["cu%d" % j])
        S.op("pool", lambda e, j=j: e.tensor_copy(out=gz_ext[:, j, 0:1], in_=gz_ext[:, j, 512:513]),
             reads=["gz%d" % j], writes=["gz%d" % j])

    qkn = [0]
    qk_pending = []

    def qk_flush():
        while qk_pending:
            qk_pending.pop(0)()

    def qk_chunk(g, fc, gain_ap, dst, dkey):
        k = qkn[0] % 2
        qkn[0] += 1
        b, pq = proj_fm(g, fc)
        S.op("act", lambda e, k=k, pq=pq: e.activation(out=qsq[k][:], in_=pq, func=AF.Square),
             reads=["ps%d" % b], writes=["qsq%d" % k])
        qk_flush()

        def stage2():
            b2 = next_bank()
            S.op("pe", lambda e, k=k, b2=b2: e.matmul(psum[b2][:, :], blk[:], qsq[k][:], start=True, stop=True),
                 reads=["qsq%d" % k, "blk"], writes=["ps%d" % b2])
            S.op("act", lambda e, k=k, b2=b2: e.activation(out=rr[k][:], in_=psum[b2][:, :], func=AF.Ln,
                                                           scale=1.0 / 64, bias=EPS),
                 reads=["ps%d" % b2], writes=["rr%d" % k])
            S.op("act", lambda e, k=k: e.activation(out=rr[k][:], in_=rr[k][:], func=AF.Exp, scale=-0.5),
                 reads=["rr%d" % k], writes=["rr%d" % k])
            S.op("dve", lambda e, k=k, pq=pq: e.scalar_tensor_tensor(
                out=dst, in0=pq, scalar=gain_ap, in1=rr[k][:], op0=ALU.mult, op1=ALU.mult),
                 reads=["ps%d" % b, "rr%d" % k, "gq", "cst"], writes=[dkey])
        qk_pending.append(stage2)

    FIRST_OWN = HALO // 512
    LAST_OWN = (HALO + OWN) // 512 - 1
    order = [0, 1, 11, 10] + list(range(2, 10))
    hpar = {g: i % 2 for i, g in enumerate(order)}

    def prefetch_pair0():
        S.op("sp", lambda e: e.dma_start(out=KT_A[:], in_=Kd[:, 0, :]),
             reads=KD_KEYS, writes=["KT_A"], dma_group="ktp0")
        S.op("sp", lambda e: e.dma_start(out=QT_A[:], in_=Qd[:, 0, :]),
             reads=QD_KEYS, writes=["QT_A"], dma_group="qtp0")
        tmap = {}
        n = 0
        for res, tiles in attn_sequences(0):
            nt = len(tiles)
            src = bass.AP(Vd_t, (res + tiles[0][0]) * 512, [[512, 128], [128 * 512, nt], [1, 128]])
            S.op("sp", lambda e, n=n, nt=nt, src=src: e.dma_start(out=Vt0[:, n:n + nt, :], in_=src),
                 reads=VD_KEYS, writes=["Vt0_t%d" % (n + i) for i in range(nt)], dma_group="vt0")
            for tl in tiles:
                tmap[(res, tl[0])] = n
                n += 1
        return 0, tmap

    emit_xload(order[0])
    emit_xnorm(order[0])
    emit_weights(0, 2)
    emit_xtrans(order[0])
    emit_xload(order[1])
    emit_xnorm(order[1])
    emit_weights(2, 8)
    S.op("pool", lambda e: e.memset(cu_ext[:], 0.0), writes=["cu_ext"])
    S.op("pool", lambda e: e.memset(gz_ext[:], 0.0), writes=["gz_ext"])
    pre_v = None
    for oi, g in enumerate(order):
        if oi + 1 < len(order) and oi > 0:
            emit_xtrans(order[oi + 1])
        spread = oi >= 4 and oi + 2 < len(order)
        if oi + 2 < len(order):
            emit_xload(order[oi + 2])
            if 0 < oi < 4:
                emit_xnorm(order[oi + 2])
        own = FIRST_OWN <= g <= LAST_OWN
        go = g - FIRST_OWN
        sl = oi % 2
        last = oi == len(order) - 1
        if g == LAST_OWN + 1:
            S.op("act", lambda e, g=g: e.activation(out=hT_tail[:, :, 0:1], in_=hT[hpar[g]][:, :, 0:1], func=AF.Copy),
                 reads=hkeys(g, 0, 1), writes=["hT_tail"])
        hb = hT[hpar[g]]
        for t in range(4):
            b = next_bank()
            for kc in range(8):
                S.op("pe", lambda e, b=b, hb=hb, kc=kc, t=t: e.matmul(
                    psum[b][:, :], hb[:, kc, 128 * t:128 * t + 128], Wbf[:, kc, 3072:3584],
                    start=(kc == 0), stop=(kc == 7)),
                     reads=wkeys(3072, 3584) + hkeys(g, 128 * t, 128 * t + 128), writes=["ps%d" % b])
            S.op("act", lambda e, b=b, t=t: e.activation(out=stV[0][:, t, :], in_=psum[b][:, :], func=AF.Copy),
                 reads=["ps%d" % b], writes=["stV0_%d" % t])
        S.op("sp", lambda e, g=g: e.dma_start(
            out=Vd[512 * g:512 * g + 512, :].rearrange("(t p) f -> p t f", p=128), in_=stV[0][:]),
             reads=["stV0_%d" % t for t in range(4)], writes=["Vd_g%d" % g] + ["stV0_%d" % t for t in range(4)],
             dma_group="stV0")
        for p in range(4):
            if own:
                qk_chunk(g, 16 + p, gq[:, 0:1], stQ[sl][:, p, :], "stQ%d_%d" % (sl, p))
        if own:
            qk_flush()
            S.op("sp", lambda e, sl=sl, go=go: e.dma_start(out=Qd[:, :, 512 * go:512 * go + 512], in_=stQ[sl][:]),
                 reads=["stQ%d_%d" % (sl, p) for p in range(4)],
                 writes=["Qd_g%d" % go] + ["stQ%d_%d" % (sl, p) for p in range(4)], dma_group="stQ%d" % sl)
        for p in range(4):
            qk_chunk(g, 20 + p, cst[:, 25:26], stK[sl][:, p, :], "stK%d_%d" % (sl, p))
        qk_flush()
        S.op("sp", lambda e, sl=sl, g=g: e.dma_start(out=Kd[:, :, 512 * g:512 * g + 512], in_=stK[sl][:]),
             reads=["stK%d_%d" % (sl, p) for p in range(4)],
             writes=["Kd_g%d" % g] + ["stK%d_%d" % (sl, p) for p in range(4)], dma_group="stK%d" % sl)
        if oi == 0:
            emit_xtrans(order[1])
            emit_xnorm(order[2])
        if last:
            pre_v = prefetch_pair0()
        if g == FIRST_OWN - 1:
            for j in range(4):
                conv_chunk(g, j, 511, 512, 1, False)
        if own:
            for j in range(4):
                conv_chunk(g, j, 0, 512, 2, True)
                conv_out(j, 512, stY[sl], 0, 1, "stY%d" % sl)
                conv_carry(j)
                if spread:
                    emit_xnorm(order[oi + 2], tiles=(j,))
            ykeys = ["stY%d_%d" % (sl, j) for j in range(4)]
            if go == 0:
                S.op("sp", lambda e, sl=sl: e.dma_start(out=Yd[:, :, 0:511], in_=stY[sl][:, :, 1:512]),
                     reads=ykeys, writes=["Yd"] + ykeys, dma_group="stY%d" % sl)
            else:
                S.op("sp", lambda e, sl=sl, go=go: e.dma_start(out=Yd[:, :, 512 * go - 1:512 * go + 511],
                                                              in_=stY[sl][:, :, :]),
                     reads=ykeys, writes=["Yd"] + ykeys, dma_group="stY%d" % sl)
        if g == LAST_OWN:
            so = 1 - sl
            for j in range(4):
                conv_chunk("tail", j, 0, 1, 2, False)
                conv_out(j, 1, stY[so], 0, 1, "stY%d" % so)
            ykeys = ["stY%d_%d" % (so, j) for j in range(4)]
            S.op("sp", lambda e, so=so: e.dma_start(out=Yd[:, :, OWN - 1:OWN], in_=stY[so][:, :, 0:1],
                                                    allow_slow_non_contiguous=True),
                 reads=ykeys, writes=["Yd"] + ykeys, dma_group="stY%d" % so)
        if own:
            for p in range(4):
                b, pz = proj_fm(g, 28 + p)
                S.op("act", lambda e, p=p, pz=pz: e.activation(out=stZ[0][:, p, :], in_=pz, func=AF.Silu),
                     reads=["ps%d" % b], writes=["stZ0_%d" % p])
            S.op("sp", lambda e, go=go: e.dma_start(out=Zd[:, :, 512 * go:512 * go + 512], in_=stZ[0][:]),
                 reads=["stZ0_%d" % p for p in range(4)], writes=["Zd"] + ["stZ0_%d" % p for p in range(4)],
                 dma_group="stZ0")

    S.barrier(lambda e: e.memset(dummy[:, 0:1], 0.0), "fenceAB")
    FB = ["fenceAB"]
    cur[0] = P0
    Yattn, _ = alloc("Yattn", [128, 4, OWN], BF16)
    Wobf, _ = alloc("Wobf", [128, 8, 1024], BF16)
    PC0 = cur[0]
    KT_B, _ = alloc("KT_B", [128, LOC], BF16)
    QT_B, _ = alloc("QT_B", [128, OWN], BF16)
    ZSp = [alloc("ZSp%d" % i, [128, OWN], BF16)[0] for i in range(2)]
    Vt1, _ = alloc("Vt1", [128, 48, 128], BF16)
    Vt = [Vt0, Vt1]
    expb, _ = alloc("expb", [128, 3, 8, 256], BF16)
    accb, _ = alloc("accb", [128, 2, OWN], F32)
    Eb = [alloc("E%d" % i, [128, 2, 512], BF16)[0] for i in range(3)]
    Pb = [alloc("P%d" % i, [128, 2, 512], BF16)[0] for i in range(3)]

    vones, _ = alloc("vones", [128, NM, 64], BF16)
    vb_b = bass.AP(vb, 0, [[NM, 128], [1, NM], [0, 64]])
    S.op("dve", lambda e: e.tensor_scalar(out=vones[:], in0=vb_b, scalar1=-1.0 / NEG, scalar2=1.0,
                                          op0=ALU.mult, op1=ALU.add),
         reads=["vb"] + FB, writes=["vones"])
    ycb_p, _ = alloc("ycb_p", [128, 4, 512], BF16)
    xr_p = [alloc("xr_p%d" % i, [128, 1024], F32)[0] for i in range(2)]
    assert cur[0] <= PF0, (cur[0], PF0)

    KTn, QTn = [KT_A, KT_B], [QT_A, QT_B]
    KTk, QTk = ["KT_A", "KT_B"], ["QT_A", "QT_B"]

    def load_pair(p, with_z=True):
        sl = p % 2
        S.op("sp", lambda e, p=p, sl=sl: e.dma_start(out=KTn[sl][:], in_=Kd[:, p, :]),
             reads=KD_KEYS + FB, writes=[KTk[sl]], dma_group="ktp%d" % sl)
        S.op("sp", lambda e, p=p, sl=sl: e.dma_start(out=QTn[sl][:], in_=Qd[:, p, :]),
             reads=QD_KEYS + FB, writes=[QTk[sl]], dma_group="qtp%d" % sl)
        if with_z:
            load_z(p)

    def load_z(p):
        S.op("sp", lambda e, p=p: e.dma_start(out=ZSp[p % 2][:], in_=Zd[:, p, :]),
             reads=["Zd"] + FB, writes=["ZSp%d" % (p % 2)], dma_group="zsp%d" % (p % 2))

    def deint_piece(res):
        S.op("act", lambda e, res=res: e.activation(out=KT_B[:, 384 * res:384 * res + 384],
                                                    in_=KT_A[:, res:res + 383 * 16 + 1:16], func=AF.Copy),
             reads=["KT_A"] + FB, writes=["KT_B"])
        S.op("act", lambda e, res=res: e.activation(out=QT_B[:, 256 * res:256 * res + 256],
                                                    in_=QT_A[:, res:res + 255 * 16 + 1:16], func=AF.Copy),
             reads=["QT_A"] + FB, writes=["QT_B"])

    vstep = [0]

    def load_vtiles(p, di):
        sl = vstep[0] % 2
        vstep[0] += 1
        d = DILS[di]
        tmap = {}
        n = 0
        for res, tiles in attn_sequences(di):
            nt = len(tiles)
            T00 = tiles[0][0]
            src = bass.AP(Vd_t, (res + d * T00) * 512 + 128 * p,
                          [[d * 512, 128], [128 * d * 512, nt], [1, 128]])
            S.op("sp", lambda e, sl=sl, n=n, nt=nt, src=src: e.dma_start(out=Vt[sl][:, n:n + nt, :], in_=src),
                 reads=VD_KEYS + FB, writes=["Vt%d_t%d" % (sl, n + i) for i in range(nt)], dma_group="vt%d" % sl)
            for (T0, useA, useB, masked) in tiles:
                tmap[(res, T0)] = n
                n += 1
        return sl, tmap

    sbank = {0: [2, 4], 1: [3, 5]}
    oslots = [0, 1, 6, 7]
    sctr = [0]
    blkctr = [0]
    mulctr = [0]
    evctr = [0]

    def attention_stages(p, di, vsl, tmap, first_dil):
        stages = []
        d = DILS[di]
        seqs = attn_sequences(di)
        if d == 16:
            chunks = []
            for r0 in range(0, 16, 2):
                for ti in range(3):
                    chunks.append([(seqs[r0][0], seqs[r0][1][ti]), (seqs[r0 + 1][0], seqs[r0 + 1][1][ti])])
            blkctr[0] = (blkctr[0] + 1) // 2 * 2
        else:
            flat = [(res, tl) for res, tiles in seqs for tl in tiles]
            chunks = [flat[i:i + 2] for i in range(0, len(flat), 2)]
        open_blocks = {}
        for ch in chunks:
            lay = []
            cpos = 0
            for i, (res, (T0, useA, useB, masked)) in enumerate(ch):
                q0 = T0 - 64 if useA else T0 + 64
                w = 128 * (int(useA) + int(useB))
                lay.append((cpos, q0, w, 0 if useA else 128))
                cpos += w
            width = cpos
            full = len(ch) == 2 and lay[0][2] == lay[1][2] and lay[0][3] == lay[1][3]

            def stage1(ch=ch, lay=lay, width=width, full=full):
                r2 = sctr[0] % 2
                r3 = sctr[0] % 3
                sctr[0] += 1
                b0 = sbank[0][r2]
                for i, (res, (T0, useA, useB, masked)) in enumerate(ch):
                    c0, q0, w, boff = lay[i]
                    for hh in range(2):
                        sb_ = sbank[hh][r2]
                        if d == 16 and False:
                            kap = KT_B[64 * hh:64 * hh + 64, 384 * res + T0:384 * res + T0 + 128]
                            qap = QT_B[64 * hh:64 * hh + 64, 256 * res + q0 - 64:256 * res + q0 - 64 + w]
                            rk = ["KT_B", "QT_B"]
                        else:
                            ks = res + d * T0
                            qs = res + d * q0 - HALO
                            kap = KTn[p % 2][64 * hh:64 * hh + 64, ks:ks + 127 * d + 1:d]
                            qap = QTn[p % 2][64 * hh:64 * hh + 64, qs:qs + (w - 1) * d + 1:d]
                            rk = [KTk[p % 2], QTk[p % 2]]
                        S.op("pe", lambda e, sb_=sb_, hh=hh, c0=c0, w=w, kap=kap, qap=qap: e.matmul(
                            psum[sb_][:, c0:c0 + w], kap, qap, start=True, stop=True,
                            tile_position=(64 * hh, 0)),
                             reads=rk + FB, writes=["ps%d" % sb_])
                Et, Pt = Eb[r3], Pb[r3]
                hb0 = (di * 8 + 2 * p) * 256
                for i in range(len(ch)):
                    c0, q0, w, boff = lay[i]
                    ei = bass.AP(psall, 512 * b0 + c0, [[4096, 128], [512, 2], [1, w]])
                    S.op("act", lambda e, Et=Et, ei=ei, c0=c0, w=w: e.activation(
                        out=Et[:, :, c0:c0 + w], in_=ei, func=AF.Exp),
                         reads=["ps%d" % b0, "ps%d" % (b0 + 1)] + FB, writes=["E%d_%d" % (r3, i)])
                    bin_ = bass.AP(expb, hb0 + boff, [[3 * 8 * 256, 128], [256, 2], [1, w]])
                    S.op("dve", lambda e, Et=Et, Pt=Pt, c0=c0, w=w, bin_=bin_: e.tensor_tensor(
                        out=Pt[:, :, c0:c0 + w], in0=Et[:, :, c0:c0 + w], in1=bin_, op=ALU.mult),
                         reads=["E%d_%d" % (r3, i), "expb%d" % di] + FB, writes=["P%d_%d" % (r3, i)])
                return r3

            plan = []
            for i, (res, (T0, useA, useB, masked)) in enumerate(ch):
                c0, q0, w, boff = lay[i]
                ts = tmap[(res, T0)]
                mi = MIDX[(di, res, T0)] if masked else None
                halves = []
                if useA:
                    halves.append((T0 - 64, c0))
                if useB:
                    halves.append((T0 + 64, c0 + (128 if useA else 0)))
                for (bstart, cc) in halves:
                    first = (res, bstart) not in open_blocks
                    if first:
                        open_blocks[(res, bstart)] = oslots[blkctr[0] % 4]
                        blkctr[0] += 1
                    ob = open_blocks[(res, bstart)]
                    if not first:
                        del open_blocks[(res, bstart)]
                    plan.append((res, ts, mi, bstart, cc, first, ob, i))

            def stage2(r, plan=plan):
                closed = []
                for (res, ts, mi, bstart, cc, first, ob, ti) in plan:
                    pso = psum[ob]
                    den_l = ones[:, 0:64] if mi is None else vones[:, mi, :]
                    Pt = Pb[r]
                    for hh in range(2):
                        S.op("pe", lambda e, pso=pso, hh=hh, ts=ts, cc=cc, Pt=Pt, first=first: e.matmul(
                            pso[64 * hh:64 * hh + 64, 0:128], Vt[vsl][:, ts, 64 * hh:64 * hh + 64],
                            Pt[:, hh, cc:cc + 128], start=first, stop=(not first), tile_position=(0, 64 * hh)),
                             reads=["P%d_%d" % (r, ti), "Vt%d_t%d" % (vsl, ts)] + FB, writes=["ps%d" % ob])
                    for hh in range(2):
                        S.op("pe", lambda e, pso=pso, hh=hh, cc=cc, Pt=Pt, first=first, den_l=den_l: e.matmul(
                            pso[64 * hh:64 * hh + 64, 128:256], den_l,
                            Pt[:, hh, cc:cc + 128], start=False, stop=(not first), skip_group_check=True,
                            tile_position=(0, 64 * hh)),
                             reads=["P%d_%d" % (r, ti), "ones", "vones"] + FB, writes=["ps%d" % ob])
                    if not first:
                        closed.append((res, bstart, ob))
                i = 0
                while i < len(closed):
                    res, bstart, ob = closed[i]
                    pair_ok = (d == 16 and i + 1 < len(closed) and closed[i + 1][0] == res + 1
                               and closed[i + 1][1] == bstart and closed[i + 1][2] == ob + 1)
                    ts0 = res + d * bstart - HALO
                    last_tok = ts0 + 127 * d + (1 if pair_ok else 0)
                    akeys = ["acc%d" % c for c in range(ts0 // 1024, last_tok // 1024 + 1)]
                    if pair_ok:
                        accv = bass.AP(accb, ts0, [[2 * OWN, 128], [1, 2], [OWN, 2], [d, 128]])
                        pv = bass.AP(psall, 512 * ob, [[4096, 128], [512, 2], [128, 2], [1, 128]])
                        rk = ["ps%d" % ob, "ps%d" % (ob + 1)]
                        i += 2
                    else:
                        accv = accb[:, :, ts0:ts0 + 127 * d + 1:d]
                        pv = psum[ob][:, 0:256].rearrange("p (t w) -> p t w", t=2)
                        rk = ["ps%d" % ob]
                        i += 1
                    if first_dil:
                        evctr[0] += 1
                        if evctr[0] % 2 == 0:
                            S.op("act", lambda e, accv=accv, pv=pv: e.activation(out=accv, in_=pv, func=AF.Copy),
                                 reads=rk + FB, writes=akeys)
                        else:
                            S.op("dve", lambda e, accv=accv, pv=pv: e.tensor_copy(out=accv, in_=pv),
                                 reads=rk + FB, writes=akeys)
                    else:
                        S.op("dve", lambda e, accv=accv, pv=pv: e.tensor_tensor(
                            out=accv, in0=accv, in1=pv, op=ALU.add),
                             reads=rk + akeys + FB, writes=akeys)
            stages.append((stage1, stage2))
        assert not open_blocks
        return stages

    def normalise(p, c):
        psl = p % 2
        cs = slice(1024 * c, 1024 * c + 1024)
        ak = ["acc%d" % c]
        S.op("act", lambda e: e.activation(out=accb[:, 1, cs], in_=accb[:, 1, cs], func=AF.Ln),
             reads=ak + FB, writes=ak)
        S.op("act", lambda e: e.activation(out=accb[:, 1, cs], in_=accb[:, 1, cs], func=AF.Exp, scale=-1.0),
             reads=ak + FB, writes=ak)
        S.op("dve", lambda e: e.tensor_tensor(out=accb[:, 0, cs], in0=accb[:, 0, cs],
                                              in1=accb[:, 1, cs], op=ALU.mult),
             reads=ak + FB, writes=ak)
        S.op("dve", lambda e: e.tensor_tensor(
            out=Yattn[:, p, cs], in0=accb[:, 0, cs], in1=ZSp[psl][:, cs], op=ALU.mult),
             reads=ak + ["ZSp%d" % psl] + FB, writes=["Yattn_p%d_c%d" % (p, c)])

    bias_exp_later = []

    def load_bias_tables(dis, defer_exp=False):
        stg = [accb[:, 0, 0:2048], accb[:, 0, 2048:4096], accb[:, 1, 0:2048]]
        sk = [["acc0", "acc1"], ["acc2", "acc3"], ["acc0", "acc1"]]
        for di in dis:
            v = stg[di].rearrange("p (h q) -> p h q", h=8)
            S.op("sp", lambda e, di=di, v=v: e.dma_start(out=v, in_=bias_d[di]),
                 reads=FB, writes=["bst%d" % di], dma_group="bst%d" % di)

            def do_exp(di=di, v=v):
                S.op("act", lambda e: e.activation(out=expb[:, di, :, :], in_=v, func=AF.Exp),
                     reads=["bst%d" % di] + FB, writes=["expb%d" % di] + sk[di])
            if defer_exp:
                bias_exp_later.append(do_exp)
            else:
                do_exp()

    wout_v = wout_d.rearrange("(kc p) n -> p kc n", p=128)

    def load_wout():
        for hf in range(2):
            S.op("pool", lambda e, hf=hf: e.dma_start(out=Wobf[:, :, 512 * hf:512 * hf + 512],
                                                      in_=wout_v[:, :, 512 * hf:512 * hf + 512]),
                 reads=FB, writes=["Wobf%d" % hf], dma_group="wold%d" % hf)

    steps = [(p, di) for p in range(4) for di in range(3)]
    vinfo = {0: pre_v}
    vstep[0] = 1
    load_bias_tables([0])
    load_bias_tables([1, 2], defer_exp=True)
    load_z(0)
    load_wout()
    pend = None
    norm_todo = []
    for sidx, (p, di) in enumerate(steps):
        if di == 0:
            deint = list(range(16))
        vsl, tmap = vinfo.pop(sidx)
        for si, (st1, st2) in enumerate(attention_stages(p, di, vsl, tmap, di == 0)):
            r = st1()
            while bias_exp_later:
                bias_exp_later.pop(0)()
            if pend is not None:
                pend[0](pend[1])
            pend = (st2, r)
            if si == 1 and di == 0 and p + 1 < 4:
                load_pair(p + 1, with_z=False)
            if si == 6 and di == 0 and p + 1 < 4:
                load_z(p + 1)
            if si == 1 and sidx + 1 < len(steps):
                vinfo[sidx + 1] = load_vtiles(*steps[sidx + 1])
            if norm_todo:
                normalise(*norm_todo.pop(0))
            if False and di < 2 and deint and si % 2 == 1:
                deint_piece(deint.pop(0))
        if di == 2:
            norm_todo = [(p, c) for c in range(4)]
    if pend is not None:
        pend[0](pend[1])
    for pc in norm_todo:
        normalise(*pc)

    NXR, NOB = 6, 4

    def c_loads(tt, fence):
        gg = tt // 4
        fk = fence
        if tt % 4 == 0:
            S.op("sp", lambda e, gg=gg: e.dma_start(out=ycb[gg % 2][:], in_=Yd[:, :, 512 * gg:512 * gg + 512]),
                 reads=["Yd"] + fk, writes=["ycb%d" % (gg % 2)], dma_group="ycb%d" % (gg % 2))
        S.op("sp", lambda e, tt=tt: e.dma_start(out=xr[tt % NXR][:],
                                                in_=xh_d[HALO + 128 * tt:HALO + 128 * tt + 128, :]),
             reads=fk, writes=["xr%d" % (tt % NXR)], dma_group="xr%d" % (tt % NXR))

    ycb = [ycb_p, None]
    xr = xr_p + [None] * (NXR - 2)
    for tt in range(2):
        c_loads(tt, [])
    S.barrier(lambda e: e.memset(dummy[:, 1:2], 0.0), "fenceBC")
    FC = ["fenceBC"]
    cur[0] = PC0
    ycb[1], _ = alloc("ycb1", [128, 4, 512], BF16)
    for i in range(2, NXR):
        xr[i], _ = alloc("xr%d" % i, [128, 1024], F32)
    ob = [alloc("ob%d" % i, [128, 1024], F32)[0] for i in range(NOB)]
    cbank = [0]

    PRE = 4
    c_loads(2, FC)
    c_loads(3, FC)
    for tt in range(32):
        gg = tt // 4
        if tt + PRE < 32:
            c_loads(tt + PRE, FC)
        mm_f = [] if gg == 0 else FC
        for hf in range(2):
            b = cbank[0] % 8
            cbank[0] += 1
            for ch in range(8):
                if ch < 4:
                    lhsT = ycb[gg % 2][:, ch, 128 * (tt % 4):128 * (tt % 4) + 128]
                    rk = ["ycb%d" % (gg % 2)]
                else:
                    lhsT = Yattn[:, ch - 4, 128 * tt:128 * tt + 128]
                    rk = ["Yattn_p%d_c%d" % (ch - 4, tt // 8)]
                S.op("pe", lambda e, b=b, lhsT=lhsT, ch=ch, hf=hf: e.matmul(
                    psum[b][:, :], lhsT, Wobf[:, ch, 512 * hf:512 * hf + 512], start=(ch == 0), stop=(ch == 7)),
                     reads=rk + ["Wobf%d" % hf] + mm_f,
                     writes=["ps%d" % b])
            S.op("dve", lambda e, b=b, tt=tt, hf=hf: e.tensor_tensor(
                out=ob[tt % NOB][:, 512 * hf:512 * hf + 512], in0=xr[tt % NXR][:, 512 * hf:512 * hf + 512],
                in1=psum[b][:, :], op=ALU.add),
                 reads=["ps%d" % b, "xr%d" % (tt % NXR)] + FC, writes=["ob%d_%d" % (tt % NOB, hf)])
        S.op("act", lambda e, tt=tt: e.dma_start(out=out_d[128 * tt:128 * tt + 128, :], in_=ob[tt % NOB][:]),
             reads=["ob%d_0" % (tt % NOB), "ob%d_1" % (tt % NOB)] + FC,
             writes=["ob%d_0" % (tt % NOB), "ob%d_1" % (tt % NOB)], dma_group="ob%d" % (tt % NOB))

    S.emit(nc, final_groups=["ob%d" % i for i in range(NOB)])
    return nc


def host_tables(rel_bias):
    a = np.arange(128)[:, None]
    b = np.arange(256)[None, :]
    rel = a - b + 64
    band = np.abs(rel) <= 64
    out = np.full((3, 128, 8, 256), NEG, np.float32)
    for di, d in enumerate(DILS):
        buckets = t5_bucket_np(np.clip(rel, -64, 64) * d)
        g = rel_bias[buckets]
        g = np.transpose(g, (0, 2, 1))
        m = np.broadcast_to(band[:, None, :], g.shape)
        out[di][m] = g[m]
    return out


_CACHE = {}


def kernel(x, norm_w, w_in, conv_w, conv_b, q_norm_w, k_norm_w, rel_bias, w_out):
    x = np.asarray(x, np.float32)
    w_in = np.ascontiguousarray(np.asarray(w_in, np.float32))
    w_out = np.ascontiguousarray(np.asarray(w_out, np.float32))
    norm_w = np.asarray(norm_w, np.float32)
    conv_w = np.asarray(conv_w, np.float32)
    conv_b = np.asarray(conv_b, np.float32)
    q_norm_w = np.asarray(q_norm_w, np.float32)
    k_norm_w = np.asarray(k_norm_w, np.float32)
    rel_bias = np.asarray(rel_bias, np.float32)

    MIDX, NM = masked_tile_index()
    cst = np.zeros((128, 32), np.float32)
    cst[:, 0:8] = norm_w.reshape(8, 128).T
    for j in range(4):
        for k in range(3):
            cst[:, 8 + 3 * j + k] = conv_w[k, 128 * j:128 * j + 128]
        cst[:, 20 + j] = conv_b[128 * j:128 * j + 128]
    cst[:, 24] = np.tile(q_norm_w, 2)
    cst[:, 25] = np.tile(k_norm_w, 2)
    biasT = host_tables(rel_bias)
    nwb = np.ascontiguousarray(np.broadcast_to(norm_w[None, :], (128, DM)))

    in_maps = []
    for c in range(NCORES):
        b = c // 4
        s0 = (c % 4) * OWN
        xh = np.zeros((LOC, DM), np.float32)
        lo = max(0, s0 - HALO)
        hi = min(SEQ, s0 + OWN + HALO)
        xh[lo - (s0 - HALO):hi - (s0 - HALO)] = x[b, lo:hi]
        vbm = np.zeros((128, NM), np.float32)
        for (di, res, T0), mi in MIDX.items():
            tl = res + DILS[di] * (T0 + np.arange(128))
            pos = s0 - HALO + tl
            vbm[:, mi] = np.where((pos >= 0) & (pos < SEQ), 0.0, NEG)
        in_maps.append({"xh": xh, "w_in": w_in, "w_out": w_out, "cst": cst, "vb": vbm, "biasT": biasT,
                        "nwb": nwb})

    if "nc" not in _CACHE:
        _CACHE["nc"] = build_program()
    nc = _CACHE["nc"]
    res = run_bass_kernel_spmd(nc, in_maps, core_ids=list(range(NCORES)))
    out = np.empty((2, SEQ, DM), np.float32)
    for c in range(NCORES):
        b = c // 4
        s0 = (c % 4) * OWN
        out[b, s0:s0 + OWN] = res.results[c]["out"]
    return out
```

```python
import contextlib
import math
import numpy as np
import concourse.bass as bass
import concourse.mybir as mybir
from concourse.bass_utils import run_bass_kernel_spmd

F32 = mybir.dt.float32
BF16 = mybir.dt.bfloat16
AF = mybir.ActivationFunctionType
ALU = mybir.AluOpType

NCORES = 8
SEQ = 16384
DM = 1024
OWN = 4096
HALO = 1024
LOC = OWN + 2 * HALO
NG = LOC // 512
NEG = -30000.0
EPS = 1e-6
DILS = (1, 4, 16)


class Sched:
    EPOCH = 12000

    def __init__(self):
        self.ops = []
        self.last_writer = {}
        self.readers = {}
        self.last_on_eng = {}
        self.last_in_group = {}

    def op(self, eng, fn, reads=(), writes=(), dma_group=None, extra_deps=()):
        idx = len(self.ops)
        deps = set(extra_deps)
        for k in reads:
            w = self.last_writer.get(k)
            if w is not None:
                deps.add(w)
        for k in writes:
            w = self.last_writer.get(k)
            if w is not None:
                deps.add(w)
            deps.update(self.readers.get(k, ()))
        deps.discard(idx)
        for k in reads:
            self.readers.setdefault(k, []).append(idx)
        for k in writes:
            self.last_writer[k] = idx
            self.readers[k] = []
        self.ops.append(dict(eng=eng, fn=fn, deps=deps, dma_group=dma_group, signal=None))
        if dma_group is None:
            self.last_on_eng[eng] = idx
        else:
            self.last_in_group[dma_group] = idx
        return idx

    def barrier(self, fn, key):
        deps = set(self.last_on_eng.values()) | set(self.last_in_group.values())
        return self.op("pool", fn, writes=[key], extra_deps=deps)

    @staticmethod
    def _skip(od, o):
        return (od["eng"] == "pe" and o["eng"] == "pe" and od["dma_group"] is None
                and o["dma_group"] is None)

    def emit(self, nc, final_groups=()):
        ops = self.ops
        needed = set()
        for o in ops:
            for d in o["deps"]:
                if not self._skip(ops[d], o):
                    needed.add(d)
        counters = {}
        ecount = {}
        group_ops = {}
        for i, o in enumerate(ops):
            if o["dma_group"] is not None:
                key = "d_" + o["dma_group"]
                counters[key] = counters.get(key, 0) + 16
                o["signal"] = (key, counters[key], 16)
                group_ops.setdefault(o["dma_group"], []).append(i)
            elif i in needed:
                n = ecount.get(o["eng"], 0)
                ecount[o["eng"]] = n + 1
                key = "e_%s_%d" % (o["eng"], n // self.EPOCH)
                counters[key] = counters.get(key, 0) + 1
                o["signal"] = (key, counters[key], 1)
        keys = sorted(counters.keys())
        with contextlib.ExitStack() as es:
            sems = {k: es.enter_context(nc.semaphore(k)) for k in keys}
            streams = {}
            for i, o in enumerate(ops):
                streams.setdefault(o["eng"], []).append(i)
            plans = {}
            import bisect
            for e, lst in streams.items():
                waited = {}
                plan = []
                for i in lst:
                    o = ops[i]
                    w = {}
                    for d in o["deps"]:
                        od = ops[d]
                        if od["signal"] is None or self._skip(od, o):
                            continue
                        if od["dma_group"] is not None:
                            gl = group_ops[od["dma_group"]]
                            j = bisect.bisect_left(gl, i) - 1
                            od = ops[gl[j]]
                        k, c, _ = od["signal"]
                        if waited.get(k, 0) >= c:
                            continue
                        w[k] = max(w.get(k, 0), c)
                    for k, c in w.items():
                        waited[k] = c
                    plan.append((i, sorted(w.items())))
                plans[e] = plan
            final = [("d_" + g, counters["d_" + g]) for g in final_groups]
            engmap = {"pe": "tensor", "act": "scalar", "dve": "vector", "pool": "gpsimd",
                      "sp": "sync"}
            with nc.Block() as block:
                for e in ["sp", "pe", "act", "dve", "pool"]:
                    plan = plans.get(e, [])

                    def body(engine, plan=plan, e=e):
                        for i, ws in plan:
                            for k, c in ws:
                                engine.wait_ge(sems[k], c)
                            o = ops[i]
                            ins = o["fn"](engine)
                            if o["signal"] is not None:
                                k, c, inc = o["signal"]
                                ins.then_inc(sems[k], inc)
                        if e == "sp":
                            for k, c in final:
                                engine.wait_ge(sems[k], c)
                    getattr(block, engmap[e])(body)
        return counters


def attn_sequences(di):
    d = DILS[di]
    seqs = []
    if d == 16:
        t0s = [0, 128, 256]
        own_lo, own_hi = 64, 320
    else:
        own_lo, own_hi = HALO // d, (HALO + OWN) // d
        t0s = [128 * j + 64 for j in range(own_lo // 128 - 1, own_hi // 128)]
    for res in range(d):
        tiles = []
        for T0 in t0s:
            useA = own_lo <= T0 - 64 and T0 + 64 <= own_hi
            useB = own_lo <= T0 + 64 and T0 + 192 <= own_hi
            lo_tok = res + d * T0
            hi_tok = res + d * (T0 + 127)
            masked = lo_tok < HALO or hi_tok >= HALO + OWN
            tiles.append((T0, useA, useB, masked))
        seqs.append((res, tiles))
    return seqs


def masked_tile_index():
    idx = {}
    n = 0
    for di in range(3):
        for res, tiles in attn_sequences(di):
            for (T0, useA, useB, masked) in tiles:
                if masked:
                    idx[(di, res, T0)] = n
                    n += 1
    return idx, n


def t5_bucket_np(rel):
    nb = 32
    half_b = nb // 2
    max_exact = half_b // 2
    ret = np.where(rel > 0, half_b, 0)
    n = np.abs(rel)
    nf = np.maximum(n, 1).astype(np.float32)
    large = max_exact + (np.log(nf / np.float32(max_exact)) / np.float32(math.log(1024 / max_exact))
                         * (half_b - max_exact)).astype(np.int32)
    large = np.minimum(large, half_b - 1)
    return ret + np.where(n < max_exact, n, large)


def build_program():
    nc = bass.Bass("TRN2", target_bir_lowering=False)
    MIDX, NM = masked_tile_index()

    xh_d = nc.dram_tensor("xh", [LOC, DM], F32, kind="ExternalInput").ap()
    win_d = nc.dram_tensor("w_in", [DM, 4096], F32, kind="ExternalInput").ap()
    wout_d = nc.dram_tensor("w_out", [DM, DM], F32, kind="ExternalInput").ap()
    cst_d = nc.dram_tensor("cst", [128, 32], F32, kind="ExternalInput").ap()
    nwb_d = nc.dram_tensor("nwb", [128, DM], F32, kind="ExternalInput").ap()
    vb_d = nc.dram_tensor("vb", [128, NM], F32, kind="ExternalInput").ap()
    bias_d = nc.dram_tensor("biasT", [3, 128, 8, 256], F32, kind="ExternalInput").ap()
    out_d = nc.dram_tensor("out", [OWN, DM], F32, kind="ExternalOutput").ap()
    Qd = nc.dram_tensor("Qd", [128, 4, OWN], BF16).ap()
    Kd = nc.dram_tensor("Kd", [128, 4, LOC], BF16).ap()
    Zd = nc.dram_tensor("Zd", [128, 4, OWN], BF16).ap()
    Yd = nc.dram_tensor("Yd", [128, 4, OWN], BF16).ap()
    Vd_t = nc.dram_tensor("Vd", [LOC, 512], BF16)
    Vd = Vd_t.ap()

    S = Sched()
    base = (nc.sbuf_base + 31) // 32 * 32
    top = nc.sbuf_top
    cur = [base]

    def alloc(name, shape, dt, at=None):
        nbytes = int(np.prod(shape[1:])) * (4 if dt == F32 else 2)
        nbytes = (nbytes + 31) // 32 * 32
        if at is None:
            off = cur[0]
            cur[0] += nbytes
        else:
            off = at
        assert off + nbytes <= top, (name, off, nbytes, top)
        return nc.alloc_sbuf_tensor_at(name, list(shape), dt, offset=off), off + nbytes

    ident, _ = alloc("ident", [128, 128], BF16)
    ones, _ = alloc("ones", [128, 128], BF16)
    blk, _ = alloc("blk", [128, 128], BF16)
    cst, _ = alloc("cst_sb", [128, 32], F32)
    vb, _ = alloc("vb_sb", [128, NM], F32)
    stat, _ = alloc("stat", [128, 64], F32)
    gq, _ = alloc("gq", [128, 2], F32)
    dummy, _ = alloc("dummy", [128, 8], F32)
    nwb, _ = alloc("nwb_sb", [128, DM], F32)
    P0 = cur[0]

    psall = nc.alloc_psum_tensor("psall", [128, 4096], F32)
    psum = [psall[:, 512 * i:512 * i + 512] for i in range(8)]

    S.op("sp", lambda e: e.dma_start(out=cst[:], in_=cst_d[:, :]), writes=["cst"], dma_group="cst")
    S.op("sp", lambda e: e.dma_start(out=vb[:], in_=vb_d[:, :]), writes=["vb"], dma_group="vbl")
    S.op("sp", lambda e: e.dma_start(out=nwb[:], in_=nwb_d[:, :]), writes=["nwb"], dma_group="nwbl")
    S.op("pool", lambda e: e.memset(ident[:], 1.0), writes=["ident"])
    S.op("pool", lambda e: e.affine_select(out=ident[:], in_=ident[:], pattern=[[-1, 128]],
                                           compare_op=ALU.is_equal, fill=0.0, base=0,
                                           channel_multiplier=1),
         reads=["ident"], writes=["ident"])
    S.op("pool", lambda e: e.memset(ones[:], 1.0), writes=["ones"])
    S.op("pool", lambda e: e.memset(blk[:], 0.0), writes=["blk"])
    S.op("pool", lambda e: e.memset(blk[0:64, 0:64], 1.0), reads=["blk"], writes=["blk"])
    S.op("pool", lambda e: e.memset(blk[64:128, 64:128], 1.0), reads=["blk"], writes=["blk"])
    S.op("dve", lambda e: e.tensor_scalar(out=gq[:, 0:1], in0=cst[:, 24:25], scalar1=0.125,
                                          scalar2=None, op0=ALU.mult),
         reads=["cst"], writes=["gq"])

    cur[0] = P0
    Wbf, _ = alloc("Wbf", [128, 8, 4096], BF16)
    xs = [alloc("xs%d" % i, [128, 1024], F32)[0] for i in range(4)]
    xn = [alloc("xn%d" % i, [128, 1024], BF16)[0] for i in range(4)]
    hT = [alloc("hT%d" % i, [128, 8, 512], BF16)[0] for i in range(2)]
    cu_ext, _ = alloc("cu_ext", [128, 4, 514], F32)
    gz_ext, _ = alloc("gz_ext", [128, 4, 513], F32)
    u_sb = [alloc("u_sb%d" % i, [128, 512], F32)[0] for i in range(2)]
    sz_sb = [alloc("sz_sb%d" % i, [128, 512], F32)[0] for i in range(2)]
    tcv = [alloc("tcv%d" % i, [128, 512], F32)[0] for i in range(2)]
    qsq = [alloc("qsq%d" % i, [128, 512], BF16)[0] for i in range(2)]
    rr = [alloc("rr%d" % i, [128, 512], F32)[0] for i in range(2)]
    stQ = [alloc("stQ%d" % i, [128, 4, 512], BF16)[0] for i in range(2)]
    stK = [alloc("stK%d" % i, [128, 4, 512], BF16)[0] for i in range(2)]
    stZ = [alloc("stZ%d" % i, [128, 4, 512], BF16)[0] for i in range(1)]
    stY = [alloc("stY%d" % i, [128, 4, 512], BF16)[0] for i in range(2)]
    stV = [alloc("stV%d" % i, [128, 4, 512], BF16)[0] for i in range(1)]
    hT_tail, _ = alloc("hT_tail", [128, 8, 16], BF16)
    PA_END = cur[0]
    PF0 = (top - (LOC + OWN + 48 * 128) * 2) // 32 * 32
    assert PA_END <= PF0, (PA_END, PF0)
    KT_A, pf1 = alloc("KT_A", [128, LOC], BF16, at=PF0)
    QT_A, pf2 = alloc("QT_A", [128, OWN], BF16, at=pf1)
    Vt0, pf3 = alloc("Vt0", [128, 48, 128], BF16, at=pf2)
    KD_KEYS = ["Kd_g%d" % g for g in range(NG)]
    QD_KEYS = ["Qd_g%d" % g for g in range(OWN // 512)]
    VD_KEYS = ["Vd_g%d" % g for g in range(NG)]

    win_v = win_d.rearrange("(kc p) n -> p kc n", p=128)
    wb_order = [6, 5, 4, 0, 2, 1, 3, 7]

    def emit_weights(lo, hi):
        for bi in range(lo, hi):
            b = wb_order[bi]
            S.op("pool", lambda e, b=b: e.dma_start(out=Wbf[:, :, 512 * b:512 * b + 512],
                                                    in_=win_v[:, :, 512 * b:512 * b + 512]),
                 writes=["Wbf_c%d" % b], dma_group="wld%d" % b)

    def wkeys(c0, c1):
        return ["Wbf_c%d" % b for b in range(c0 // 512, (c1 - 1) // 512 + 1)]

    S.op("pool", lambda e: e.memset(cu_ext[:], 0.0), writes=["cu_ext"])
    S.op("pool", lambda e: e.memset(gz_ext[:], 0.0), writes=["gz_ext"])

    acc_rot = [0]

    def next_bank():
        b = 2 + acc_rot[0] % 6
        acc_rot[0] += 1
        return b

    xtile_ctr = [0]
    xinfo = {}

    def emit_xload(g):
        for t in range(4):
            n = xtile_ctr[0]
            xtile_ctr[0] += 1
            row0 = 512 * g + 128 * t
            S.op("sp", lambda e, n=n, row0=row0: e.dma_start(out=xs[n % 4][:], in_=xh_d[row0:row0 + 128, :]),
                 writes=["xs%d" % (n % 4)], dma_group="xs%d" % (n % 4))
            xinfo[(g, t)] = n

    def emit_xnorm(g, tiles=(0, 1, 2, 3)):
        for t in tiles:
            n = xinfo[(g, t)]
            xb = xs[n % 4]
            xnb = xn[n % 4]
            sc = n % 16
            S.op("act", lambda e, xb=xb, xnb=xnb, sc=sc: e.activation(out=xnb[:], in_=xb[:], func=AF.Square,
                                                                      accum_out=stat[:, sc:sc + 1]),
                 reads=["xs%d" % (n % 4)], writes=["xn%d" % (n % 4), "ss%d" % sc])
            S.op("act", lambda e, sc=sc: e.activation(out=stat[:, 16 + sc:17 + sc], in_=stat[:, sc:sc + 1],
                                                      func=AF.Ln, scale=1.0 / DM, bias=EPS),
                 reads=["ss%d" % sc], writes=["sr%d" % sc])
            S.op("act", lambda e, sc=sc: e.activation(out=stat[:, 32 + sc:33 + sc], in_=stat[:, 16 + sc:17 + sc],
                                                      func=AF.Exp, scale=-0.5),
                 reads=["sr%d" % sc], writes=["si%d" % sc])
            S.op("dve", lambda e, xb=xb, xnb=xnb, sc=sc: e.scalar_tensor_tensor(
                out=xnb[:], in0=xb[:], scalar=stat[:, 32 + sc:33 + sc], in1=nwb[:], op0=ALU.mult, op1=ALU.mult),
                 reads=["xs%d" % (n % 4), "si%d" % sc, "nwb"], writes=["xn%d" % (n % 4)])

    def emit_xtrans(g):
        hb = hT[hpar[g]]
        for t in range(4):
            n = xinfo[(g, t)]
            xnb = xn[n % 4]
            tb = n % 2
            pt = psum[tb][:].bitcast(BF16)
            for kc in range(8):
                S.op("pe", lambda e, pt=pt, xnb=xnb, kc=kc: e.transpose(
                    out=pt[:, 128 * kc:128 * kc + 128], in_=xnb[:, 128 * kc:128 * kc + 128], identity=ident[:]),
                     reads=["xn%d" % (n % 4), "ident"], writes=["pst%d" % tb])
            eng = "act" if t == 0 else "dve"
            outv = hb[:, :, 128 * t:128 * t + 128]
            inv = pt.rearrange("p (k n) -> p k n", k=8)
            if eng == "act":
                f = lambda e, outv=outv, inv=inv: e.activation(out=outv, in_=inv, func=AF.Copy)
            else:
                f = lambda e, outv=outv, inv=inv: e.tensor_copy(out=outv, in_=inv)
            S.op(eng, f, reads=["pst%d" % tb], writes=["hT%d_t%d" % (hpar[g], t)])

    def hkeys(g, c0=0, c1=512):
        return ["hT%d_t%d" % (hpar[g], t) for t in range(c0 // 128, (c1 - 1) // 128 + 1)]

    def proj_fm(g, fc, c0=0, c1=512):
        b = next_bank()
        if g == "tail":
            hb, hk = hT_tail, ["hT_tail"]
        else:
            hb, hk = hT[hpar[g]], hkeys(g, c0, c1)
        n = c1 - c0
        for kc in range(8):
            S.op("pe", lambda e, b=b, hb=hb, kc=kc, fc=fc, c0=c0, c1=c1, n=n: e.matmul(
                psum[b][:, 0:n], Wbf[:, kc, 128 * fc:128 * fc + 128], hb[:, kc, c0:c1],
                start=(kc == 0), stop=(kc == 7)),
                 reads=wkeys(128 * fc, 128 * fc + 128) + hk, writes=["ps%d" % b])
        return b, psum[b][:, 0:n]

    cvn = [0]

    def conv_chunk(g, j, c0, c1, e0, store_cols):
        n = c1 - c0
        k = cvn[0] % 2
        cvn[0] += 1
        bu, pu = proj_fm(g, j, c0, c1)
        bgc, pgc = proj_fm(g, 8 + j, c0, c1)
        S.op("act", lambda e, k=k, pu=pu, n=n: e.activation(out=u_sb[k][:, 0:n], in_=pu, func=AF.Copy),
             reads=["ps%d" % bu], writes=["u_sb%d" % k])
        S.op("dve", lambda e, k=k, pgc=pgc, j=j, n=n, e0=e0: e.tensor_tensor(
            out=cu_ext[:, j, e0:e0 + n], in0=u_sb[k][:, 0:n], in1=pgc, op=ALU.mult),
             reads=["u_sb%d" % k, "ps%d" % bgc, "cu_ext"], writes=["cu%d" % j])
        if not store_cols:
            return
        bgb, pgb = proj_fm(g, 4 + j, c0, c1)
        bzc, pzc = proj_fm(g, 12 + j, c0, c1)
        S.op("act", lambda e, k=k, pzc=pzc, n=n: e.activation(out=sz_sb[k][:, 0:n], in_=pzc, func=AF.Silu),
             reads=["ps%d" % bzc], writes=["sz_sb%d" % k])
        S.op("dve", lambda e, k=k, pgb=pgb, j=j, n=n: e.tensor_tensor(
            out=gz_ext[:, j, 1:1 + n], in0=sz_sb[k][:, 0:n], in1=pgb, op=ALU.mult),
             reads=["sz_sb%d" % k, "ps%d" % bgb, "gz_ext"], writes=["gz%d" % j])

    def conv_out(j, n, ybuf, ycol0, ext_lo, ykey):
        k = cvn[0] % 2
        cvn[0] += 1
        w0 = cst[:, 8 + 3 * j:9 + 3 * j]
        w1 = cst[:, 9 + 3 * j:10 + 3 * j]
        w2 = cst[:, 10 + 3 * j:11 + 3 * j]
        bb = cst[:, 20 + j:21 + j]
        a = ext_lo
        S.op("pool", lambda e, k=k, j=j, a=a, n=n: e.tensor_scalar(
            out=tcv[k][:, 0:n], in0=cu_ext[:, j, a:a + n], scalar1=w1, scalar2=bb, op0=ALU.mult, op1=ALU.add),
             reads=["cu%d" % j, "cu_ext", "cst"], writes=["tcv%d" % k])
        S.op("dve", lambda e, k=k, j=j, a=a, n=n: e.scalar_tensor_tensor(
            out=tcv[k][:, 0:n], in0=cu_ext[:, j, a - 1:a - 1 + n], scalar=w0, in1=tcv[k][:, 0:n],
            op0=ALU.mult, op1=ALU.add),
             reads=["cu%d" % j, "tcv%d" % k], writes=["tcv%d" % k])
        S.op("dve", lambda e, k=k, j=j, a=a, n=n: e.scalar_tensor_tensor(
            out=tcv[k][:, 0:n], in0=cu_ext[:, j, a + 1:a + 1 + n], scalar=w2, in1=tcv[k][:, 0:n],
            op0=ALU.mult, op1=ALU.add),
             reads=["cu%d" % j, "tcv%d" % k], writes=["tcv%d" % k])
        S.op("pool", lambda e, k=k, j=j, a=a, n=n: e.tensor_tensor(
            out=ybuf[:, j, ycol0:ycol0 + n], in0=tcv[k][:, 0:n], in1=gz_ext[:, j, a - 1:a - 1 + n], op=ALU.mult),
             reads=["tcv%d" % k, "gz%d" % j, "gz_ext"], writes=[ykey + "_%d" % j])

    def conv_carry(j):
        S.op("pool", lambda e, j=j: e.tensor_copy(out=cu_ext[:, j, 0:2], in_=cu_ext[:, j, 512:514]),
             reads=["cu%d" % j], writes=["cu%d" % j])
        S.op("pool", lambda e, j=j: e.tensor_copy(out=gz_ext[:, j, 0:1], in_=gz_ext[:, j, 512:513]),
             reads=["gz%d" % j], writes=["gz%d" % j])

    qkn = [0]
    qk_pending = []

    def qk_flush():
        while qk_pending:
            qk_pending.pop(0)()

    def qk_chunk(g, fc, gain_ap, dst, dkey):
        k = qkn[0] % 2
        qkn[0] += 1
        b, pq = proj_fm(g, fc)
        S.op("act", lambda e, k=k, pq=pq: e.activation(out=qsq[k][:], in_=pq, func=AF.Square),
             reads=["ps%d" % b], writes=["qsq%d" % k])
        qk_flush()

        def stage2():
            b2 = next_bank()
            S.op("pe", lambda e, k=k, b2=b2: e.matmul(psum[b2][:, :], blk[:], qsq[k][:], start=True, stop=True),
                 reads=["qsq%d" % k, "blk"], writes=["ps%d" % b2])
            S.op("act", lambda e, k=k, b2=b2: e.activation(out=rr[k][:], in_=psum[b2][:, :], func=AF.Ln,
                                                           scale=1.0 / 64, bias=EPS),
                 reads=["ps%d" % b2], writes=["rr%d" % k])
            S.op("act", lambda e, k=k: e.activation(out=rr[k][:], in_=rr[k][:], func=AF.Exp, scale=-0.5),
                 reads=["rr%d" % k], writes=["rr%d" % k])
            S.op("dve", lambda e, k=k, pq=pq: e.scalar_tensor_tensor(
                out=dst, in0=pq, scalar=gain_ap, in1=rr[k][:], op0=ALU.mult, op1=ALU.mult),
                 reads=["ps%d" % b, "rr%d" % k, "gq", "cst"], writes=[dkey])
        qk_pending.append(stage2)

    FIRST_OWN = HALO // 512
    LAST_OWN = (HALO + OWN) // 512 - 1
    order = [0, 1, 11, 10] + list(range(2, 10))
    hpar = {g: i % 2 for i, g in enumerate(order)}

    def prefetch_pair0():
        S.op("sp", lambda e: e.dma_start(out=KT_A[:], in_=Kd[:, 0, :]),
             reads=KD_KEYS, writes=["KT_A"], dma_group="ktp0")
        S.op("sp", lambda e: e.dma_start(out=QT_A[:], in_=Qd[:, 0, :]),
             reads=QD_KEYS, writes=["QT_A"], dma_group="qtp0")
        tmap = {}
        n = 0
        for res, tiles in attn_sequences(0):
            nt = len(tiles)
            src = bass.AP(Vd_t, (res + tiles[0][0]) * 512, [[512, 128], [128 * 512, nt], [1, 128]])
            S.op("sp", lambda e, n=n, nt=nt, src=src: e.dma_start(out=Vt0[:, n:n + nt, :], in_=src),
                 reads=VD_KEYS, writes=["Vt0_t%d" % (n + i) for i in range(nt)], dma_group="vt0")
            for tl in tiles:
                tmap[(res, tl[0])] = n
                n += 1
        return 0, tmap

    emit_xload(order[0])
    emit_xnorm(order[0])
    emit_weights(0, 2)
    emit_xtrans(order[0])
    emit_xload(order[1])
    emit_xnorm(order[1])
    emit_weights(2, 8)
    pre_v = None
    for oi, g in enumerate(order):
        if oi + 1 < len(order) and oi > 0:
            emit_xtrans(order[oi + 1])
        spread = oi >= 4 and oi + 2 < len(order)
        if oi + 2 < len(order):
            emit_xload(order[oi + 2])
            if 0 < oi < 4:
                emit_xnorm(order[oi + 2])
        own = FIRST_OWN <= g <= LAST_OWN
        go = g - FIRST_OWN
        sl = oi % 2
        last = oi == len(order) - 1
        if g == LAST_OWN + 1:
            S.op("act", lambda e, g=g: e.activation(out=hT_tail[:, :, 0:1], in_=hT[hpar[g]][:, :, 0:1], func=AF.Copy),
                 reads=hkeys(g, 0, 1), writes=["hT_tail"])
        hb = hT[hpar[g]]
        for t in range(4):
            b = next_bank()
            for kc in range(8):
                S.op("pe", lambda e, b=b, hb=hb, kc=kc, t=t: e.matmul(
                    psum[b][:, :], hb[:, kc, 128 * t:128 * t + 128], Wbf[:, kc, 3072:3584],
                    start=(kc == 0), stop=(kc == 7)),
                     reads=wkeys(3072, 3584) + hkeys(g, 128 * t, 128 * t + 128), writes=["ps%d" % b])
            S.op("act", lambda e, b=b, t=t: e.activation(out=stV[0][:, t, :], in_=psum[b][:, :], func=AF.Copy),
                 reads=["ps%d" % b], writes=["stV0_%d" % t])
        S.op("sp", lambda e, g=g: e.dma_start(
            out=Vd[512 * g:512 * g + 512, :].rearrange("(t p) f -> p t f", p=128), in_=stV[0][:]),
             reads=["stV0_%d" % t for t in range(4)], writes=["Vd_g%d" % g] + ["stV0_%d" % t for t in range(4)],
             dma_group="stV0")
        for p in range(4):
            if own:
                qk_chunk(g, 16 + p, gq[:, 0:1], stQ[sl][:, p, :], "stQ%d_%d" % (sl, p))
        if own:
            qk_flush()
            S.op("sp", lambda e, sl=sl, go=go: e.dma_start(out=Qd[:, :, 512 * go:512 * go + 512], in_=stQ[sl][:]),
                 reads=["stQ%d_%d" % (sl, p) for p in range(4)],
                 writes=["Qd_g%d" % go] + ["stQ%d_%d" % (sl, p) for p in range(4)], dma_group="stQ%d" % sl)
        for p in range(4):
            qk_chunk(g, 20 + p, cst[:, 25:26], stK[sl][:, p, :], "stK%d_%d" % (sl, p))
        qk_flush()
        S.op("sp", lambda e, sl=sl, g=g: e.dma_start(out=Kd[:, :, 512 * g:512 * g + 512], in_=stK[sl][:]),
             reads=["stK%d_%d" % (sl, p) for p in range(4)],
             writes=["Kd_g%d" % g] + ["stK%d_%d" % (sl, p) for p in range(4)], dma_group="stK%d" % sl)
        if oi == 0:
            emit_xtrans(order[1])
            emit_xnorm(order[2])
        if last:
            pre_v = prefetch_pair0()
        if g == FIRST_OWN - 1:
            for j in range(4):
                conv_chunk(g, j, 511, 512, 1, False)
        if own:
            for j in range(4):
                conv_chunk(g, j, 0, 512, 2, True)
                conv_out(j, 512, stY[sl], 0, 1, "stY%d" % sl)
                conv_carry(j)
                if spread:
                    emit_xnorm(order[oi + 2], tiles=(j,))
            ykeys = ["stY%d_%d" % (sl, j) for j in range(4)]
            if go == 0:
                S.op("sp", lambda e, sl=sl: e.dma_start(out=Yd[:, :, 0:511], in_=stY[sl][:, :, 1:512]),
                     reads=ykeys, writes=["Yd"] + ykeys, dma_group="stY%d" % sl)
            else:
                S.op("sp", lambda e, sl=sl, go=go: e.dma_start(out=Yd[:, :, 512 * go - 1:512 * go + 511],
                                                              in_=stY[sl][:, :, :]),
                     reads=ykeys, writes=["Yd"] + ykeys, dma_group="stY%d" % sl)
        if g == LAST_OWN:
            so = 1 - sl
            for j in range(4):
                conv_chunk("tail", j, 0, 1, 2, False)
                conv_out(j, 1, stY[so], 0, 1, "stY%d" % so)
            ykeys = ["stY%d_%d" % (so, j) for j in range(4)]
            S.op("sp", lambda e, so=so: e.dma_start(out=Yd[:, :, OWN - 1:OWN], in_=stY[so][:, :, 0:1],
                                                    allow_slow_non_contiguous=True),
                 reads=ykeys, writes=["Yd"] + ykeys, dma_group="stY%d" % so)
        if own:
            for p in range(4):
                b, pz = proj_fm(g, 28 + p)
                S.op("act", lambda e, p=p, pz=pz: e.activation(out=stZ[0][:, p, :], in_=pz, func=AF.Silu),
                     reads=["ps%d" % b], writes=["stZ0_%d" % p])
            S.op("sp", lambda e, go=go: e.dma_start(out=Zd[:, :, 512 * go:512 * go + 512], in_=stZ[0][:]),
                 reads=["stZ0_%d" % p for p in range(4)], writes=["Zd"] + ["stZ0_%d" % p for p in range(4)],
                 dma_group="stZ0")

    S.barrier(lambda e: e.memset(dummy[:, 0:1], 0.0), "fenceAB")
    FB = ["fenceAB"]
    cur[0] = P0
    Yattn, _ = alloc("Yattn", [128, 4, OWN], BF16)
    Wobf, _ = alloc("Wobf", [128, 8, 1024], BF16)
    PC0 = cur[0]
    KT_B, _ = alloc("KT_B", [128, LOC], BF16)
    QT_B, _ = alloc("QT_B", [128, OWN], BF16)
    ZSp = [alloc("ZSp%d" % i, [128, OWN], BF16)[0] for i in range(2)]
    Vt1, _ = alloc("Vt1", [128, 48, 128], BF16)
    Vt = [Vt0, Vt1]
    expb, _ = alloc("expb", [128, 3, 8, 256], BF16)
    accb, _ = alloc("accb", [128, 2, OWN], F32)
    Eb = [alloc("E%d" % i, [128, 2, 512], BF16)[0] for i in range(3)]
    Pb = [alloc("P%d" % i, [128, 2, 512], BF16)[0] for i in range(3)]

    vones, _ = alloc("vones", [128, NM, 64], BF16)
    vb_b = bass.AP(vb, 0, [[NM, 128], [1, NM], [0, 64]])
    S.op("dve", lambda e: e.tensor_scalar(out=vones[:], in0=vb_b, scalar1=-1.0 / NEG, scalar2=1.0,
                                          op0=ALU.mult, op1=ALU.add),
         reads=["vb"] + FB, writes=["vones"])
    ycb_p, _ = alloc("ycb_p", [128, 4, 512], BF16)
    xr_p = [alloc("xr_p%d" % i, [128, 1024], F32)[0] for i in range(2)]
    assert cur[0] <= PF0, (cur[0], PF0)

    KTn, QTn = [KT_A, KT_B], [QT_A, QT_B]
    KTk, QTk = ["KT_A", "KT_B"], ["QT_A", "QT_B"]

    def load_pair(p, with_z=True):
        sl = p % 2
        S.op("sp", lambda e, p=p, sl=sl: e.dma_start(out=KTn[sl][:], in_=Kd[:, p, :]),
             reads=KD_KEYS + FB, writes=[KTk[sl]], dma_group="ktp%d" % sl)
        S.op("sp", lambda e, p=p, sl=sl: e.dma_start(out=QTn[sl][:], in_=Qd[:, p, :]),
             reads=QD_KEYS + FB, writes=[QTk[sl]], dma_group="qtp%d" % sl)
        if with_z:
            load_z(p)

    def load_z(p):
        S.op("sp", lambda e, p=p: e.dma_start(out=ZSp[p % 2][:], in_=Zd[:, p, :]),
             reads=["Zd"] + FB, writes=["ZSp%d" % (p % 2)], dma_group="zsp%d" % (p % 2))

    def deint_piece(res):
        S.op("act", lambda e, res=res: e.activation(out=KT_B[:, 384 * res:384 * res + 384],
                                                    in_=KT_A[:, res:res + 383 * 16 + 1:16], func=AF.Copy),
             reads=["KT_A"] + FB, writes=["KT_B"])
        S.op("act", lambda e, res=res: e.activation(out=QT_B[:, 256 * res:256 * res + 256],
                                                    in_=QT_A[:, res:res + 255 * 16 + 1:16], func=AF.Copy),
             reads=["QT_A"] + FB, writes=["QT_B"])

    vstep = [0]

    def load_vtiles(p, di):
        sl = vstep[0] % 2
        vstep[0] += 1
        d = DILS[di]
        tmap = {}
        n = 0
        for res, tiles in attn_sequences(di):
            nt = len(tiles)
            T00 = tiles[0][0]
            src = bass.AP(Vd_t, (res + d * T00) * 512 + 128 * p,
                          [[d * 512, 128], [128 * d * 512, nt], [1, 128]])
            S.op("sp", lambda e, sl=sl, n=n, nt=nt, src=src: e.dma_start(out=Vt[sl][:, n:n + nt, :], in_=src),
                 reads=VD_KEYS + FB, writes=["Vt%d_t%d" % (sl, n + i) for i in range(nt)], dma_group="vt%d" % sl)
            for (T0, useA, useB, masked) in tiles:
                tmap[(res, T0)] = n
                n += 1
        return sl, tmap

    sbank = {0: [2, 4], 1: [3, 5]}
    oslots = [0, 1, 6, 7]
    sctr = [0]
    blkctr = [0]
    mulctr = [0]
    evctr = [0]

    def attention_stages(p, di, vsl, tmap, first_dil):
        stages = []
        d = DILS[di]
        seqs = attn_sequences(di)
        if d == 16:
            chunks = []
            for r0 in range(0, 16, 2):
                for ti in range(3):
                    chunks.append([(seqs[r0][0], seqs[r0][1][ti]), (seqs[r0 + 1][0], seqs[r0 + 1][1][ti])])
            blkctr[0] = (blkctr[0] + 1) // 2 * 2
        else:
            flat = [(res, tl) for res, tiles in seqs for tl in tiles]
            chunks = [flat[i:i + 2] for i in range(0, len(flat), 2)]
        open_blocks = {}
        for ch in chunks:
            lay = []
            cpos = 0
            for i, (res, (T0, useA, useB, masked)) in enumerate(ch):
                q0 = T0 - 64 if useA else T0 + 64
                w = 128 * (int(useA) + int(useB))
                lay.append((cpos, q0, w, 0 if useA else 128))
                cpos += w
            width = cpos
            full = len(ch) == 2 and lay[0][2] == lay[1][2] and lay[0][3] == lay[1][3]

            def stage1(ch=ch, lay=lay, width=width, full=full):
                r2 = sctr[0] % 2
                r3 = sctr[0] % 3
                sctr[0] += 1
                b0 = sbank[0][r2]
                for i, (res, (T0, useA, useB, masked)) in enumerate(ch):
                    c0, q0, w, boff = lay[i]
                    for hh in range(2):
                        sb_ = sbank[hh][r2]
                        if d == 16 and False:
                            kap = KT_B[64 * hh:64 * hh + 64, 384 * res + T0:384 * res + T0 + 128]
                            qap = QT_B[64 * hh:64 * hh + 64, 256 * res + q0 - 64:256 * res + q0 - 64 + w]
                            rk = ["KT_B", "QT_B"]
                        else:
                            ks = res + d * T0
                            qs = res + d * q0 - HALO
                            kap = KTn[p % 2][64 * hh:64 * hh + 64, ks:ks + 127 * d + 1:d]
                            qap = QTn[p % 2][64 * hh:64 * hh + 64, qs:qs + (w - 1) * d + 1:d]
                            rk = [KTk[p % 2], QTk[p % 2]]
                        S.op("pe", lambda e, sb_=sb_, hh=hh, c0=c0, w=w, kap=kap, qap=qap: e.matmul(
                            psum[sb_][:, c0:c0 + w], kap, qap, start=True, stop=True,
                            tile_position=(64 * hh, 0)),
                             reads=rk + FB, writes=["ps%d" % sb_])
                Et, Pt = Eb[r3], Pb[r3]
                hb0 = (di * 8 + 2 * p) * 256
                for i in range(len(ch)):
                    c0, q0, w, boff = lay[i]
                    ei = bass.AP(psall, 512 * b0 + c0, [[4096, 128], [512, 2], [1, w]])
                    S.op("act", lambda e, Et=Et, ei=ei, c0=c0, w=w: e.activation(
                        out=Et[:, :, c0:c0 + w], in_=ei, func=AF.Exp),
                         reads=["ps%d" % b0, "ps%d" % (b0 + 1)] + FB, writes=["E%d_%d" % (r3, i)])
                    bin_ = bass.AP(expb, hb0 + boff, [[3 * 8 * 256, 128], [256, 2], [1, w]])
                    S.op("dve", lambda e, Et=Et, Pt=Pt, c0=c0, w=w, bin_=bin_: e.tensor_tensor(
                        out=Pt[:, :, c0:c0 + w], in0=Et[:, :, c0:c0 + w], in1=bin_, op=ALU.mult),
                         reads=["E%d_%d" % (r3, i), "expb%d" % di] + FB, writes=["P%d_%d" % (r3, i)])
                return r3

            plan = []
            for i, (res, (T0, useA, useB, masked)) in enumerate(ch):
                c0, q0, w, boff = lay[i]
                ts = tmap[(res, T0)]
                mi = MIDX[(di, res, T0)] if masked else None
                halves = []
                if useA:
                    halves.append((T0 - 64, c0))
                if useB:
                    halves.append((T0 + 64, c0 + (128 if useA else 0)))
                for (bstart, cc) in halves:
                    first = (res, bstart) not in open_blocks
                    if first:
                        open_blocks[(res, bstart)] = oslots[blkctr[0] % 4]
                        blkctr[0] += 1
                    ob = open_blocks[(res, bstart)]
                    if not first:
                        del open_blocks[(res, bstart)]
                    plan.append((res, ts, mi, bstart, cc, first, ob, i))

            def stage2(r, plan=plan):
                closed = []
                for (res, ts, mi, bstart, cc, first, ob, ti) in plan:
                    pso = psum[ob]
                    den_l = ones[:, 0:64] if mi is None else vones[:, mi, :]
                    Pt = Pb[r]
                    for hh in range(2):
                        S.op("pe", lambda e, pso=pso, hh=hh, ts=ts, cc=cc, Pt=Pt, first=first: e.matmul(
                            pso[64 * hh:64 * hh + 64, 0:128], Vt[vsl][:, ts, 64 * hh:64 * hh + 64],
                            Pt[:, hh, cc:cc + 128], start=first, stop=(not first), tile_position=(0, 64 * hh)),
                             reads=["P%d_%d" % (r, ti), "Vt%d_t%d" % (vsl, ts)] + FB, writes=["ps%d" % ob])
                    for hh in range(2):
                        S.op("pe", lambda e, pso=pso, hh=hh, cc=cc, Pt=Pt, first=first, den_l=den_l: e.matmul(
                            pso[64 * hh:64 * hh + 64, 128:256], den_l,
                            Pt[:, hh, cc:cc + 128], start=False, stop=(not first), skip_group_check=True,
                            tile_position=(0, 64 * hh)),
                             reads=["P%d_%d" % (r, ti), "ones", "vones"] + FB, writes=["ps%d" % ob])
                    if not first:
                        closed.append((res, bstart, ob))
                i = 0
                while i < len(closed):
                    res, bstart, ob = closed[i]
                    pair_ok = (d == 16 and i + 1 < len(closed) and closed[i + 1][0] == res + 1
                               and closed[i + 1][1] == bstart and closed[i + 1][2] == ob + 1)
                    ts0 = res + d * bstart - HALO
                    last_tok = ts0 + 127 * d + (1 if pair_ok else 0)
                    akeys = ["acc%d" % c for c in range(ts0 // 1024, last_tok // 1024 + 1)]
                    if pair_ok:
                        accv = bass.AP(accb, ts0, [[2 * OWN, 128], [1, 2], [OWN, 2], [d, 128]])
                        pv = bass.AP(psall, 512 * ob, [[4096, 128], [512, 2], [128, 2], [1, 128]])
                        rk = ["ps%d" % ob, "ps%d" % (ob + 1)]
                        i += 2
                    else:
                        accv = accb[:, :, ts0:ts0 + 127 * d + 1:d]
                        pv = psum[ob][:, 0:256].rearrange("p (t w) -> p t w", t=2)
                        rk = ["ps%d" % ob]
                        i += 1
                    if first_dil:
                        evctr[0] += 1
                        if evctr[0] % 2 == 0:
                            S.op("act", lambda e, accv=accv, pv=pv: e.activation(out=accv, in_=pv, func=AF.Copy),
                                 reads=rk + FB, writes=akeys)
                        else:
                            S.op("dve", lambda e, accv=accv, pv=pv: e.tensor_copy(out=accv, in_=pv),
                                 reads=rk + FB, writes=akeys)
                    else:
                        S.op("dve", lambda e, accv=accv, pv=pv: e.tensor_tensor(
                            out=accv, in0=accv, in1=pv, op=ALU.add),
                             reads=rk + akeys + FB, writes=akeys)
            stages.append((stage1, stage2))
        assert not open_blocks
        return stages

    def normalise(p, c):
        psl = p % 2
        cs = slice(1024 * c, 1024 * c + 1024)
        ak = ["acc%d" % c]
        S.op("act", lambda e: e.activation(out=accb[:, 1, cs], in_=accb[:, 1, cs], func=AF.Ln),
             reads=ak + FB, writes=ak)
        S.op("act", lambda e: e.activation(out=accb[:, 1, cs], in_=accb[:, 1, cs], func=AF.Exp, scale=-1.0),
             reads=ak + FB, writes=ak)
        S.op("dve", lambda e: e.tensor_tensor(out=accb[:, 0, cs], in0=accb[:, 0, cs],
                                              in1=accb[:, 1, cs], op=ALU.mult),
             reads=ak + FB, writes=ak)
        S.op("dve", lambda e: e.tensor_tensor(
            out=Yattn[:, p, cs], in0=accb[:, 0, cs], in1=ZSp[psl][:, cs], op=ALU.mult),
             reads=ak + ["ZSp%d" % psl] + FB, writes=["Yattn_p%d_c%d" % (p, c)])

    bias_exp_later = []

    def load_bias_tables(dis, defer_exp=False):
        stg = [accb[:, 0, 0:2048], accb[:, 0, 2048:4096], accb[:, 1, 0:2048]]
        sk = [["acc0", "acc1"], ["acc2", "acc3"], ["acc0", "acc1"]]
        for di in dis:
            v = stg[di].rearrange("p (h q) -> p h q", h=8)
            S.op("sp", lambda e, di=di, v=v: e.dma_start(out=v, in_=bias_d[di]),
                 reads=FB, writes=["bst%d" % di], dma_group="bst%d" % di)

            def do_exp(di=di, v=v):
                S.op("act", lambda e: e.activation(out=expb[:, di, :, :], in_=v, func=AF.Exp),
                     reads=["bst%d" % di] + FB, writes=["expb%d" % di] + sk[di])
            if defer_exp:
                bias_exp_later.append(do_exp)
            else:
                do_exp()

    wout_v = wout_d.rearrange("(kc p) n -> p kc n", p=128)

    def load_wout():
        for hf in range(2):
            S.op("pool", lambda e, hf=hf: e.dma_start(out=Wobf[:, :, 512 * hf:512 * hf + 512],
                                                      in_=wout_v[:, :, 512 * hf:512 * hf + 512]),
                 reads=FB, writes=["Wobf%d" % hf], dma_group="wold%d" % hf)

    steps = [(p, di) for p in range(4) for di in range(3)]
    vinfo = {0: pre_v}
    vstep[0] = 1
    load_bias_tables([0])
    load_bias_tables([1, 2], defer_exp=True)
    load_z(0)
    load_wout()
    pend = None
    norm_todo = []
    for sidx, (p, di) in enumerate(steps):
        if di == 0:
            deint = list(range(16))
        vsl, tmap = vinfo.pop(sidx)
        for si, (st1, st2) in enumerate(attention_stages(p, di, vsl, tmap, di == 0)):
            r = st1()
            while bias_exp_later:
                bias_exp_later.pop(0)()
            if pend is not None:
                pend[0](pend[1])
            pend = (st2, r)
            if si == 1 and di == 0 and p + 1 < 4:
                load_pair(p + 1, with_z=False)
            if si == 6 and di == 0 and p + 1 < 4:
                load_z(p + 1)
            if si == 1 and sidx + 1 < len(steps):
                vinfo[sidx + 1] = load_vtiles(*steps[sidx + 1])
            if norm_todo:
                normalise(*norm_todo.pop(0))
            if False and di < 2 and deint and si % 2 == 1:
                deint_piece(deint.pop(0))
        if di == 2:
            norm_todo = [(p, c) for c in range(4)]
    if pend is not None:
        pend[0](pend[1])
    for pc in norm_todo:
        normalise(*pc)

    NXR, NOB = 6, 4

    def c_loads(tt, fence):
        gg = tt // 4
        fk = fence
        if tt % 4 == 0:
            S.op("sp", lambda e, gg=gg: e.dma_start(out=ycb[gg % 2][:], in_=Yd[:, :, 512 * gg:512 * gg + 512]),
                 reads=["Yd"] + fk, writes=["ycb%d" % (gg % 2)], dma_group="ycb%d" % (gg % 2))
        S.op("sp", lambda e, tt=tt: e.dma_start(out=xr[tt % NXR][:],
                                                in_=xh_d[HALO + 128 * tt:HALO + 128 * tt + 128, :]),
             reads=fk, writes=["xr%d" % (tt % NXR)], dma_group="xr%d" % (tt % NXR))

    ycb = [ycb_p, None]
    xr = xr_p + [None] * (NXR - 2)
    for tt in range(2):
        c_loads(tt, [])
    S.barrier(lambda e: e.memset(dummy[:, 1:2], 0.0), "fenceBC")
    FC = ["fenceBC"]
    cur[0] = PC0
    ycb[1], _ = alloc("ycb1", [128, 4, 512], BF16)
    for i in range(2, NXR):
        xr[i], _ = alloc("xr%d" % i, [128, 1024], F32)
    ob = [alloc("ob%d" % i, [128, 1024], F32)[0] for i in range(NOB)]
    cbank = [0]

    PRE = 4
    c_loads(2, FC)
    c_loads(3, FC)
    for tt in range(32):
        gg = tt // 4
        if tt + PRE < 32:
            c_loads(tt + PRE, FC)
        mm_f = [] if gg == 0 else FC
        for hf in range(2):
            b = cbank[0] % 8
            cbank[0] += 1
            for ch in range(8):
                if ch < 4:
                    lhsT = ycb[gg % 2][:, ch, 128 * (tt % 4):128 * (tt % 4) + 128]
                    rk = ["ycb%d" % (gg % 2)]
                else:
                    lhsT = Yattn[:, ch - 4, 128 * tt:128 * tt + 128]
                    rk = ["Yattn_p%d_c%d" % (ch - 4, tt // 8)]
                S.op("pe", lambda e, b=b, lhsT=lhsT, ch=ch, hf=hf: e.matmul(
                    psum[b][:, :], lhsT, Wobf[:, ch, 512 * hf:512 * hf + 512], start=(ch == 0), stop=(ch == 7)),
                     reads=rk + ["Wobf%d" % hf] + mm_f,
                     writes=["ps%d" % b])
            S.op("dve", lambda e, b=b, tt=tt, hf=hf: e.tensor_tensor(
                out=ob[tt % NOB][:, 512 * hf:512 * hf + 512], in0=xr[tt % NXR][:, 512 * hf:512 * hf + 512],
                in1=psum[b][:, :], op=ALU.add),
                 reads=["ps%d" % b, "xr%d" % (tt % NXR)] + FC, writes=["ob%d_%d" % (tt % NOB, hf)])
        S.op("act", lambda e, tt=tt: e.dma_start(out=out_d[128 * tt:128 * tt + 128, :], in_=ob[tt % NOB][:]),
             reads=["ob%d_0" % (tt % NOB), "ob%d_1" % (tt % NOB)] + FC,
             writes=["ob%d_0" % (tt % NOB), "ob%d_1" % (tt % NOB)], dma_group="ob%d" % (tt % NOB))

    S.emit(nc, final_groups=["ob%d" % i for i in range(NOB)])
    return nc


def host_tables(rel_bias):
    a = np.arange(128)[:, None]
    b = np.arange(256)[None, :]
    rel = a - b + 64
    band = np.abs(rel) <= 64
    out = np.full((3, 128, 8, 256), NEG, np.float32)
    for di, d in enumerate(DILS):
        buckets = t5_bucket_np(np.clip(rel, -64, 64) * d)
        g = rel_bias[buckets]
        g = np.transpose(g, (0, 2, 1))
        m = np.broadcast_to(band[:, None, :], g.shape)
        out[di][m] = g[m]
    return out


_CACHE = {}


def kernel(x, norm_w, w_in, conv_w, conv_b, q_norm_w, k_norm_w, rel_bias, w_out):
    x = np.asarray(x, np.float32)
    w_in = np.ascontiguousarray(np.asarray(w_in, np.float32))
    w_out = np.ascontiguousarray(np.asarray(w_out, np.float32))
    norm_w = np.asarray(norm_w, np.float32)
    conv_w = np.asarray(conv_w, np.float32)
    conv_b = np.asarray(conv_b, np.float32)
    q_norm_w = np.asarray(q_norm_w, np.float32)
    k_norm_w = np.asarray(k_norm_w, np.float32)
    rel_bias = np.asarray(rel_bias, np.float32)

    MIDX, NM = masked_tile_index()
    cst = np.zeros((128, 32), np.float32)
    cst[:, 0:8] = norm_w.reshape(8, 128).T
    for j in range(4):
        for k in range(3):
            cst[:, 8 + 3 * j + k] = conv_w[k, 128 * j:128 * j + 128]
        cst[:, 20 + j] = conv_b[128 * j:128 * j + 128]
    cst[:, 24] = np.tile(q_norm_w, 2)
    cst[:, 25] = np.tile(k_norm_w, 2)
    biasT = host_tables(rel_bias)
    nwb = np.ascontiguousarray(np.broadcast_to(norm_w[None, :], (128, DM)))

    in_maps = []
    for c in range(NCORES):
        b = c // 4
        s0 = (c % 4) * OWN
        xh = np.zeros((LOC, DM), np.float32)
        lo = max(0, s0 - HALO)
        hi = min(SEQ, s0 + OWN + HALO)
        xh[lo - (s0 - HALO):hi - (s0 - HALO)] = x[b, lo:hi]
        vbm = np.zeros((128, NM), np.float32)
        for (di, res, T0), mi in MIDX.items():
            tl = res + DILS[di] * (T0 + np.arange(128))
            pos = s0 - HALO + tl
            vbm[:, mi] = np.where((pos >= 0) & (pos < SEQ), 0.0, NEG)
        in_maps.append({"xh": xh, "w_in": w_in, "w_out": w_out, "cst": cst, "vb": vbm, "biasT": biasT,
                        "nwb": nwb})

    if "nc" not in _CACHE:
        _CACHE["nc"] = build_program()
    nc = _CACHE["nc"]
    res = run_bass_kernel_spmd(nc, in_maps, core_ids=list(range(NCORES)))
    out = np.empty((2, SEQ, DM), np.float32)
    for c in range(NCORES):
        b = c // 4
        s0 = (c % 4) * OWN
        out[b, s0:s0 + OWN] = res.results[c]["out"]
    return out
```
